# Optimizing a Trainium2 kernel written in Bass

```python
import math
import jax
import jax.numpy as jnp
from jax import lax
import numpy as np

D_MODEL = 2048
BATCH = 2
SEQ = 16384
DEPTH = 4

HEAD_DIM = 128
BRANCH_WIDTH = 3 * D_MODEL // 8
N_BRANCH = 4
BLOCK = 128
NEG_INF = -1e30
DIL_GROUPS = ((128, 1), (512, 4), (2048, 16))
A_HEADS = BRANCH_WIDTH // HEAD_DIM
C_Q_HEADS = BRANCH_WIDTH // HEAD_DIM
C_KV_HEADS = 2
C_WINDOW = 128
LRU_WIDTH = BRANCH_WIDTH
LRU_BLOCKS = LRU_WIDTH // 128
LRU_C = 8.0
CONV_WIDTH = 4
SSM_INNER = BRANCH_WIDTH
SSM_HEAD_DIM = 64
SSM_HEADS = SSM_INNER // SSM_HEAD_DIM
SSM_GROUPS = 2
SSM_STATE = 128
SSM_CHUNK = 128
SSM_CONV_DIM = SSM_INNER + 2 * SSM_GROUPS * SSM_STATE
N_BUCKETS = 32
T5_MAX_DIST = 2048
N_BIAS_HEADS = len(DIL_GROUPS) * A_HEADS + C_Q_HEADS
FFN_HIDDEN = ((8 * D_MODEL // 3 + 255) // 256) * 256
A_QKV = 3 * len(DIL_GROUPS) * A_HEADS * HEAD_DIM
IN_SPLITS = (A_QKV, C_Q_HEADS * HEAD_DIM, C_KV_HEADS * HEAD_DIM, C_KV_HEADS * HEAD_DIM,
             LRU_WIDTH, LRU_WIDTH, SSM_INNER, SSM_CONV_DIM, SSM_HEADS, N_BRANCH * D_MODEL)
IN_WIDTH = sum(IN_SPLITS)

kernel_name = "hybrid_gated_parallel_mixers"


def rms_norm(x, g, eps=1e-6):
    x32 = x.astype(jnp.float32)
    return x32 * lax.rsqrt(jnp.mean(jnp.square(x32), axis=-1, keepdims=True) + eps) * g.astype(jnp.float32)


def t5_bucket(dist):
    max_exact = N_BUCKETS // 2
    safe = np.maximum(dist, 1).astype(np.float32)
    large = max_exact + (np.log(safe / max_exact) / np.log(T5_MAX_DIST / max_exact)
                         * (N_BUCKETS - max_exact)).astype(np.int32)
    return np.where(dist < max_exact, dist, np.minimum(large, N_BUCKETS - 1)).astype(np.int32)


def band_tables(max_dist, dil, n_blocks):
    qi = np.arange(BLOCK)[:, None]
    kj = np.arange(2 * BLOCK)[None, :]
    dist = BLOCK + qi - kj
    valid = np.broadcast_to((dist >= 0) & (dist <= max_dist), (n_blocks, BLOCK, 2 * BLOCK)).copy()
    valid[0, :, :BLOCK] = False
    bucket = t5_bucket(np.clip(dist, 0, None) * dil)
    return valid, bucket


def banded_attention(q, k, v, bias_cols, max_dist, dil, sink=None):
    n, length, hk, g, dh = q.shape
    nb = length // BLOCK
    valid, bucket = band_tables(max_dist, dil, nb)
    qb = q.reshape(n, nb, BLOCK, hk, g, dh)

    def band(t):
        tb = t.reshape(n, nb, BLOCK, hk, dh)
        prev = jnp.pad(tb, ((0, 0), (1, 0), (0, 0), (0, 0), (0, 0)))[:, :-1]
        return jnp.concatenate([prev, tb], axis=2)

    kw, vw = band(k), band(v)
    s = jnp.einsum('nbqhgd,nbshd->nbhgqs', qb, kw)
    bias = bias_cols.astype(jnp.float32)[jnp.asarray(bucket)]
    bias = bias.reshape(BLOCK, 2 * BLOCK, hk, g).transpose(2, 3, 0, 1)
    s = jnp.where(jnp.asarray(valid)[None, :, None, None], s + bias, NEG_INF)
    m = jnp.max(s, axis=-1)
    if sink is not None:
        m = jnp.maximum(m, sink[:, :, None])
    p = jnp.exp(s - m[..., None])
    l = jnp.sum(p, axis=-1)
    if sink is not None:
        l = l + jnp.exp(sink[:, :, None] - m)
    o = jnp.einsum('nbhgqs,nbshd->nbqhgd', p, vw) / jnp.moveaxis(l, -1, 2)[..., None]
    o = o.reshape(n, length, hk, g, dh)
    m = jnp.moveaxis(m, -1, 2).reshape(n, length, hk, g)
    l = jnp.moveaxis(l, -1, 2).reshape(n, length, hk, g)
    return o, m, l


def to_phase(t, dil, padded_len):
    b, s, h, dh = t.shape
    t = t.reshape(b, s // dil, dil, h, dh).transpose(0, 2, 1, 3, 4).reshape(b * dil, s // dil, h, dh)
    return jnp.pad(t, ((0, 0), (0, padded_len - s // dil), (0, 0), (0, 0)))


def from_phase(t, b, s, dil):
    rest = t.shape[2:]
    t = t[:, :s // dil].reshape(b, dil, s // dil, *rest)
    return t.swapaxes(1, 2).reshape(b, s, *rest)


def dilated_attention(qkv, qk_gain, bias_table):
    b, s, _ = qkv.shape
    qkv = qkv.astype(jnp.float32).reshape(b, s, 3, len(DIL_GROUPS), A_HEADS, HEAD_DIM)
    q = rms_norm(qkv[:, :, 0], qk_gain[0]) * (HEAD_DIM ** -0.5)
    k = rms_norm(qkv[:, :, 1], qk_gain[1])
    v = qkv[:, :, 2]
    outs, ms, ls = [], [], []
    for gi, (window, dil) in enumerate(DIL_GROUPS):
        sub_len = s // dil
        padded = -(-sub_len // BLOCK) * BLOCK
        cols = bias_table[:, gi * A_HEADS:(gi + 1) * A_HEADS]
        o, m, l = banded_attention(to_phase(q[:, :, gi], dil, padded)[:, :, :, None],
                                   to_phase(k[:, :, gi], dil, padded),
                                   to_phase(v[:, :, gi], dil, padded),
                                   cols, window // dil, dil)
        outs.append(from_phase(o[:, :, :, 0], b, s, dil))
        ms.append(from_phase(m[..., 0], b, s, dil))
        ls.append(from_phase(l[..., 0], b, s, dil))
    m_all = jnp.stack(ms)
    w = jnp.stack(ls) * jnp.exp(m_all - jnp.max(m_all, axis=0))
    alpha = w / jnp.sum(w, axis=0)
    o = jnp.einsum('gbsh,gbshd->bshd', alpha, jnp.stack(outs))
    return o.reshape(b, s, A_HEADS * HEAD_DIM)


def sliding_gqa(q, k, v, qk_gain, sinks, bias_table):
    b, s, _ = q.shape
    g = C_Q_HEADS // C_KV_HEADS
    q = rms_norm(q.reshape(b, s, C_KV_HEADS, g, HEAD_DIM), qk_gain[0]) * (HEAD_DIM ** -0.5)
    k = rms_norm(k.reshape(b, s, C_KV_HEADS, HEAD_DIM), qk_gain[1])
    v = v.astype(jnp.float32).reshape(b, s, C_KV_HEADS, HEAD_DIM)
    cols = bias_table[:, len(DIL_GROUPS) * A_HEADS:]
    o, _, _ = banded_attention(q, k, v, cols, C_WINDOW - 1, 1,
                               sinks.astype(jnp.float32).reshape(C_KV_HEADS, g))
    return o.reshape(b, s, C_Q_HEADS * HEAD_DIM)


def causal_dwconv(x, w, bias):
    ch = x.shape[-1]
    y = lax.conv_general_dilated(x, w.astype(x.dtype)[:, None, :], window_strides=(1,),
                                 padding=((CONV_WIDTH - 1, 0),),
                                 dimension_numbers=('NWC', 'WIO', 'NWC'),
                                 feature_group_count=ch)
    return y + bias.astype(x.dtype)


def _lin_combine(c1, c2):
    a1, u1 = c1
    a2, u2 = c2
    return a1 * a2, a2 * u1 + u2


def rglru_branch(xr, gr, conv_w, conv_b, gate_w, gate_b, lam):
    b, s, _ = xr.shape
    xc = causal_dwconv(xr, conv_w, conv_b).astype(jnp.float32)
    xb = xc.reshape(b, s, LRU_BLOCKS, LRU_WIDTH // LRU_BLOCKS)
    gates = jnp.einsum('bshi,nhio->nbsho', xb, gate_w.astype(jnp.float32)).reshape(2, b, s, LRU_WIDTH)
    gates = gates + gate_b.astype(jnp.float32)[:, None, None, :]
    r = jax.nn.sigmoid(gates[0])
    i = jax.nn.sigmoid(gates[1])
    log_a = -LRU_C * r * jax.nn.softplus(-lam.astype(jnp.float32))
    a = jnp.exp(log_a)
    u = jnp.sqrt(-jnp.expm1(2.0 * log_a)) * (i * xc)
    h = lax.associative_scan(_lin_combine, (a, u), axis=1)[1]
    return h * jax.nn.gelu(gr.astype(jnp.float32))


def ssd_chunked(x, a, bm, cm):
    b, s, g, e, p = x.shape
    c, q = s // SSM_CHUNK, SSM_CHUNK
    x = x.reshape(b, c, q, g, e, p)
    a = a.reshape(b, c, q, g, e).transpose(0, 1, 3, 4, 2)
    bm = bm.reshape(b, c, q, g, -1)
    cm = cm.reshape(b, c, q, g, -1)
    a_cs = jnp.cumsum(a, axis=-1)
    causal = np.tril(np.ones((q, q), dtype=bool))
    seg = a_cs[..., :, None] - a_cs[..., None, :]
    decay = jnp.exp(jnp.where(causal, seg, -jnp.inf))
    cb = jnp.einsum('bclgn,bcsgn->bcgls', cm, bm)
    y_diag = jnp.einsum('bcgels,bcsgep->bclgep', cb[:, :, :, None] * decay, x)
    decay_to_end = jnp.exp(a_cs[..., -1:] - a_cs)
    states = jnp.einsum('bclgn,bcgel,bclgep->bcgepn', bm, decay_to_end, x)
    chunk_decay = jnp.exp(a_cs[..., -1])

    def step(h, inp):
        st, dec = inp
        return h * dec[..., None, None] + st, h

    h0 = jnp.zeros((b, g, e, p, bm.shape[-1]), jnp.float32)
    _, prev = lax.scan(step, h0, (jnp.moveaxis(states, 1, 0), jnp.moveaxis(chunk_decay, 1, 0)))
    prev = jnp.moveaxis(prev, 0, 1)
    y_off = jnp.einsum('bclgn,bcgepn,bcgel->bclgep', cm, prev, jnp.exp(a_cs))
    return (y_diag + y_off).reshape(b, s, g, e, p)


def ssd_branch(z, xbc, dt, conv_w, conv_b, dt_bias, a_log, d_skip, norm_w):
    b, s, _ = z.shape
    e = SSM_HEADS // SSM_GROUPS
    xbc = jax.nn.silu(causal_dwconv(xbc, conv_w, conv_b).astype(jnp.float32))
    xs, bm, cm = jnp.split(xbc, [SSM_INNER, SSM_INNER + SSM_GROUPS * SSM_STATE], axis=-1)
    xs = xs.reshape(b, s, SSM_GROUPS, e, SSM_HEAD_DIM)
    bm = bm.reshape(b, s, SSM_GROUPS, SSM_STATE)
    cm = cm.reshape(b, s, SSM_GROUPS, SSM_STATE)
    dt = jax.nn.softplus(dt.astype(jnp.float32) + dt_bias.astype(jnp.float32)).reshape(b, s, SSM_GROUPS, e)
    a = -jnp.exp(a_log.astype(jnp.float32)).reshape(SSM_GROUPS, e)
    y = ssd_chunked(xs * dt[..., None], a * dt, bm, cm)
    y = y + d_skip.astype(jnp.float32).reshape(SSM_GROUPS, e)[:, :, None] * xs
    y = y.reshape(b, s, SSM_INNER) * jax.nn.silu(z.astype(jnp.float32))
    y = rms_norm(y.reshape(b, s, SSM_GROUPS, -1), norm_w.reshape(SSM_GROUPS, -1), eps=1e-5)
    return y.reshape(b, s, SSM_INNER)


def setup_inputs(seed: int = 0) -> dict:
    key = jax.random.key(seed)
    ks = jax.random.split(key, 24)

    def nrm(k, shape, scale):
        return scale * jax.random.normal(k, shape, jnp.float32)

    u_lru = jax.random.uniform(ks[11], (DEPTH, LRU_WIDTH), jnp.float32, 0.9, 0.999)
    a_base = u_lru ** (1.0 / LRU_C)
    dt0 = jnp.exp(jax.random.uniform(ks[14], (DEPTH, SSM_HEADS), jnp.float32,
                                     math.log(1e-3), math.log(1e-1)))
    return {
        "x": nrm(ks[0], (BATCH, SEQ, D_MODEL), 1.0),
        "rel_bias_table": nrm(ks[1], (N_BUCKETS, N_BIAS_HEADS), 0.3),
        "mix_norm": 1.0 + nrm(ks[2], (DEPTH, D_MODEL), 0.05),
        "w_in": nrm(ks[3], (DEPTH, D_MODEL, IN_WIDTH), D_MODEL ** -0.5),
        "qk_norm_a": 1.0 + nrm(ks[4], (DEPTH, 2, HEAD_DIM), 0.05),
        "qk_norm_c": 1.0 + nrm(ks[5], (DEPTH, 2, HEAD_DIM), 0.05),
        "sinks_c": nrm(ks[6], (DEPTH, C_Q_HEADS), 1.0),
        "conv_lru_w": nrm(ks[7], (DEPTH, CONV_WIDTH, LRU_WIDTH), CONV_WIDTH ** -0.5),
        "conv_lru_b": nrm(ks[8], (DEPTH, LRU_WIDTH), 0.02),
        "lru_gate_w": nrm(ks[9], (DEPTH, 2, LRU_BLOCKS, LRU_WIDTH // LRU_BLOCKS, LRU_WIDTH // LRU_BLOCKS),
                          (LRU_WIDTH // LRU_BLOCKS) ** -0.5),
        "lru_gate_b": nrm(ks[10], (DEPTH, 2, LRU_WIDTH), 0.02),
        "lru_lambda": jnp.log(a_base) - jnp.log1p(-a_base),
        "ssm_conv_w": nrm(ks[12], (DEPTH, CONV_WIDTH, SSM_CONV_DIM), CONV_WIDTH ** -0.5),
        "ssm_conv_b": nrm(ks[13], (DEPTH, SSM_CONV_DIM), 0.02),
        "ssm_dt_bias": dt0 + jnp.log(-jnp.expm1(-dt0)),
        "ssm_a_log": jnp.log(jax.random.uniform(ks[15], (DEPTH, SSM_HEADS), jnp.float32, 1.0, 16.0)),
        "ssm_d": 1.0 + nrm(ks[16], (DEPTH, SSM_HEADS), 0.1),
        "ssm_norm": 1.0 + nrm(ks[17], (DEPTH, SSM_INNER), 0.05),
        "w_branch": nrm(ks[18], (DEPTH, N_BRANCH, BRANCH_WIDTH, D_MODEL), BRANCH_WIDTH ** -0.5),
        "w_out": nrm(ks[19], (DEPTH, D_MODEL, D_MODEL), D_MODEL ** -0.5),
        "ffn_norm": 1.0 + nrm(ks[20], (DEPTH, D_MODEL), 0.05),
        "w_ffn_in": nrm(ks[21], (DEPTH, D_MODEL, 2 * FFN_HIDDEN), D_MODEL ** -0.5),
        "w_ffn_out": nrm(ks[22], (DEPTH, FFN_HIDDEN, D_MODEL), FFN_HIDDEN ** -0.5),
    }


def reference(x, rel_bias_table, mix_norm, w_in, qk_norm_a, qk_norm_c, sinks_c, conv_lru_w,
              conv_lru_b, lru_gate_w, lru_gate_b, lru_lambda, ssm_conv_w, ssm_conv_b, ssm_dt_bias,
              ssm_a_log, ssm_d, ssm_norm, w_branch, w_out, ffn_norm, w_ffn_in, w_ffn_out):
    b, s, _ = x.shape
    split_idx = [int(i) for i in np.cumsum(IN_SPLITS)[:-1]]
    for layer in range(DEPTH):
        h = rms_norm(x, mix_norm[layer]).astype(x.dtype)
        proj = h @ w_in[layer]
        (a_qkv, c_q, c_k, c_v, lru_x, lru_g, ssm_z, ssm_xbc, ssm_dt,
         gate_logits) = jnp.split(proj, split_idx, axis=-1)
        branches = (
            dilated_attention(a_qkv, qk_norm_a[layer], rel_bias_table),
            rglru_branch(lru_x, lru_g, conv_lru_w[layer], conv_lru_b[layer], lru_gate_w[layer],
                         lru_gate_b[layer], lru_lambda[layer]),
            sliding_gqa(c_q, c_k, c_v, qk_norm_c[layer], sinks_c[layer], rel_bias_table),
            ssd_branch(ssm_z, ssm_xbc, ssm_dt, ssm_conv_w[layer], ssm_conv_b[layer],
                       ssm_dt_bias[layer], ssm_a_log[layer], ssm_d[layer], ssm_norm[layer]),
        )
        gates = jax.nn.sigmoid(gate_logits.astype(jnp.float32)).reshape(b, s, N_BRANCH, D_MODEL)
        merged = gates[:, :, 0] * (branches[0].astype(x.dtype) @ w_branch[layer, 0]).astype(jnp.float32)
        for bi in range(1, N_BRANCH):
            merged = merged + gates[:, :, bi] * (branches[bi].astype(x.dtype)
                                                 @ w_branch[layer, bi]).astype(jnp.float32)
        x = x + merged.astype(x.dtype) @ w_out[layer]
        hf = rms_norm(x, ffn_norm[layer]).astype(x.dtype)
        gate, up = jnp.split(hf @ w_ffn_in[layer], 2, axis=-1)
        x = x + (jax.nn.silu(gate) * up) @ w_ffn_out[layer]
    return x
```

```python
import contextlib
import numpy as np
import concourse.bass as bass
import concourse.mybir as mybir
from concourse.bass_utils import run_bass_kernel_spmd

F32 = mybir.dt.float32
BF16 = mybir.dt.bfloat16
AF = mybir.ActivationFunctionType
ALU = mybir.AluOpType
AX = mybir.AxisListType

SEG = 30000
DMA_SEG = 3500


class Ev:
    __slots__ = ("kind", "key", "idx", "sem", "val")

    def __init__(self, kind, key, idx, sem, val):
        self.kind, self.key, self.idx, self.sem, self.val = kind, key, idx, sem, val


class DSem:
    _n = 0

    def __init__(self, handle):
        self.h = handle
        self.cnt = 0
        DSem._n += 1
        self.id = DSem._n


class T:
    _n = 0

    def __init__(self, h, name):
        self.h = h
        self.name = name
        self.last_w = None
        self.readers = {}
        self.dsem = None
        T._n += 1
        self.id = T._n

    def __getitem__(self, idx):
        return self.h[idx]


class FW:
    def __init__(self, nc):
        self.nc = nc
        self.E = {"pe": nc.tensor, "act": nc.scalar, "dve": nc.vector,
                  "pool": nc.gpsimd, "sp": nc.sync}
        self.cnt = {k: 0 for k in self.E}
        self.esems = {k: [] for k in self.E}
        self.seen = {k: {} for k in self.E}
        self.pe_pending = []
        self.ptiles = []
        self.phtiles = []
        self.free_dsems = []
        self.stack = None
        self.uid = 0
        self.n_inst = 0

    def sbp(self, name, shape, dtype=F32):
        t = T(self.nc.alloc_sbuf_tensor(name, list(shape), dtype), name)
        self.ptiles.append(t)
        return t

    def psp(self, name, shape, dtype=F32):
        t = T(self.nc.alloc_psum_tensor(name, list(shape), dtype), name)
        self.ptiles.append(t)
        return t

    def begin(self):
        assert self.stack is None
        self.stack = contextlib.ExitStack()
        self.phtiles = []

    def sb(self, name, shape, dtype=F32):
        self.uid += 1
        nm = f"{name}_{self.uid}"
        h = self.stack.enter_context(self.nc.sbuf_tensor(nm, list(shape), dtype))
        t = T(h, nm)
        self.phtiles.append(t)
        return t

    def ring(self, name, n, shape, dtype=F32):
        return [self.sb(f"{name}{i}", shape, dtype) for i in range(n)]

    def end(self):
        self.barrier()
        for t in self.phtiles:
            if t.dsem is not None:
                self.free_dsems.append(t.dsem)
                t.dsem = None
        self.phtiles = []
        self.stack.close()
        self.stack = None

    def _latest(self, ek):
        idx = self.cnt[ek]
        s = (idx - 1) // SEG
        return Ev("eng", ek, idx, self.esems[ek][s], (idx - 1) % SEG + 1)

    def _new_eng_event(self, ek):
        self.cnt[ek] += 1
        s = (self.cnt[ek] - 1) // SEG
        while len(self.esems[ek]) <= s:
            self.esems[ek].append(self.nc.alloc_semaphore(f"e_{ek}_{len(self.esems[ek])}"))
        return self._latest(ek)

    def _wait(self, ek, evs):
        eng = self.E[ek]
        seen = self.seen[ek]
        best = {}
        for ev in evs:
            if ev is None:
                continue
            if ev.kind == "eng" and ev.key == ek and ek == "pe":
                continue
            if seen.get(ev.key, 0) >= ev.idx:
                continue
            b = best.get(ev.key)
            if b is None or b.idx < ev.idx:
                best[ev.key] = ev
        for key, ev in best.items():
            eng.wait_ge(ev.sem, ev.val)
            seen[key] = ev.idx

    def op(self, ek, build, reads=(), writes=(), track=True):
        evs = []
        rd = list(reads)
        if ek == "pe" and track and self.pe_pending:
            rd = rd + self.pe_pending
            self.pe_pending = []
        for t in rd:
            evs.append(t.last_w)
        for t in writes:
            evs.append(t.last_w)
            evs.extend(t.readers.values())
        self._wait(ek, evs)
        inst = build(self.E[ek])
        self.n_inst += 1
        if not track:
            assert ek == "pe"
            self.pe_pending.extend(rd)
            return inst
        ev = self._new_eng_event(ek)
        inst.then_inc(ev.sem, 1)
        for t in rd:
            t.readers[ek] = ev
        for t in writes:
            t.last_w = ev
            t.readers = {}
        return inst

    def _dsem(self, t):
        if t.dsem is not None and t.dsem.cnt >= DMA_SEG:
            t.dsem = None
        if t.dsem is None:
            while self.free_dsems:
                d = self.free_dsems.pop()
                if d.cnt < DMA_SEG:
                    t.dsem = d
                    break
            if t.dsem is None:
                t.dsem = DSem(self.nc.alloc_semaphore(f"d{DSem._n}"))
        return t.dsem

    def dma(self, qk, out, in_, tr=None, tw=None, **kw):
        t = tw if tw is not None else tr
        evs = []
        if tw is not None:
            evs.append(tw.last_w)
            evs.extend(tw.readers.values())
        if tr is not None:
            evs.append(tr.last_w)
        self._wait(qk, evs)
        inst = self.E[qk].dma_start(out=out, in_=in_, **kw)
        self.n_inst += 1
        d = self._dsem(t)
        d.cnt += 1
        inst.then_inc(d.h, 16)
        ev = Ev("dma", ("d", d.id), d.cnt, d.h, 16 * d.cnt)
        if tw is not None:
            tw.last_w = ev
            tw.readers = {}
        if tr is not None:
            tr.readers[ev.key] = ev
        return inst

    def barrier(self):
        assert not self.pe_pending
        evs = []
        for ek in self.E:
            if self.cnt[ek] > 0:
                evs.append(self._latest(ek))
        for t in self.ptiles + self.phtiles:
            if t.last_w is not None and t.last_w.kind == "dma":
                evs.append(t.last_w)
            for ev in t.readers.values():
                if ev.kind == "dma":
                    evs.append(ev)
        for ek in self.E:
            self._wait(ek, [e for e in evs if not (e.kind == "eng" and e.key == ek)])


D = 2048
NIN = 19980
FF = 5632
C_QA, C_KA, C_VA = 0, 2304, 4608
C_QC, C_KC, C_VC = 6912, 7680, 7936
C_LX, C_LG, C_Z, C_XBC, C_DT, C_G = 8192, 8960, 9728, 10496, 11776, 11788
DILS = ((128, 1), (512, 4), (2048, 16))
R_MIX, R_FFN, R_QA, R_KA, R_QC, R_KC, R_SINK, R_DTB, R_ALOG, R_DSK, R_SNORM, NR = \
    0, 2048, 4096, 4224, 4352, 4480, 4608, 4614, 4626, 4638, 4650, 5418
P_LCW, P_LCB, P_LGB, P_LAM, P_SCW, P_SCB, NP_ = 0, 24, 30, 42, 48, 88, 98


def _t5_bucket(dist):
    max_exact = 16
    safe = np.maximum(dist, 1).astype(np.float32)
    large = max_exact + (np.log(safe / max_exact) / np.log(2048 / max_exact) * 16).astype(np.int32)
    return np.where(dist < max_exact, dist, np.minimum(large, 31)).astype(np.int32)


def _bias_tables(tab):
    qi = np.arange(128)[:, None]
    kj = np.arange(256)[None, :]
    dist = 128 + qi - kj
    out = np.empty((128, 24, 256), np.float32)
    for gi, (win, dil) in enumerate(DILS):
        valid = (dist >= 0) & (dist <= win // dil)
        bk = _t5_bucket(np.clip(dist, 0, None) * dil)
        for h in range(6):
            out[:, gi * 6 + h, :] = np.where(valid, tab[bk, gi * 6 + h], np.float32(-1e30))
    valid = (dist >= 0) & (dist <= 127)
    bk = _t5_bucket(np.clip(dist, 0, None))
    for h in range(6):
        out[:, 18 + h, :] = np.where(valid, tab[bk, 18 + h], np.float32(-1e30))
    return out


class K:
    def __init__(self, S, L, taps=()):
        self.S, self.L, self.taps = S, L, set(taps)
        nc = self.nc = bass.Bass("TRN2", target_bir_lowering=False)
        self.fw = FW(nc)

        def din(name, shape, dt=F32):
            return nc.dram_tensor(name, list(shape), dt, kind="ExternalInput").ap()

        def dsc(name, shape, dt=F32):
            kind = "ExternalOutput" if name in self.taps else "Internal"
            return nc.dram_tensor(name, list(shape), dt, kind=kind).ap()

        self.x_in = din("x", [S, D])
        self.w_in = din("w_in", [L, D, NIN])
        self.w_br = din("w_branch", [L, 4, 768, D])
        self.w_out = din("w_out", [L, D, D])
        self.w_f1 = din("w_ffn_in", [L, D, 2 * FF])
        self.w_f2 = din("w_ffn_out", [L, FF, D])
        self.lgw = din("lru_gate_w", [L, 2, 6, 128, 128])
        self.prow = din("prow", [L, NR])
        self.ppar = din("ppar", [L, 128, NP_])
        self.btab = din("btab", [128, 24, 256])
        self.out = nc.dram_tensor("out", [S, D], F32, kind="ExternalOutput").ap()
        self.wb_in = [dsc(f"wb_in{l}", [D, NIN], BF16) for l in range(L)]
        self.wb_br = [dsc(f"wb_br{l}", [4, 768, D], BF16) for l in range(L)]
        self.wb_out = [dsc(f"wb_out{l}", [D, D], BF16) for l in range(L)]
        self.wb_f1 = [dsc(f"wb_f1{l}", [D, 2 * FF], BF16) for l in range(L)]
        self.wb_f2 = [dsc(f"wb_f2{l}", [FF, D], BF16) for l in range(L)]
        self.xres = self.out
        self.qa = dsc("qa", [S, 2304], BF16)
        self.ka = dsc("ka", [S, 2304], BF16)
        self.va = dsc("va", [S, 2304], BF16)
        self.qc = dsc("qc", [S, 768], BF16)
        self.kc = dsc("kc", [S, 256], BF16)
        self.vc = dsc("vc", [S, 256], BF16)
        self.lrux = dsc("lrux", [768, 4 + S])
        self.lrug = dsc("lrug", [768, S])
        self.z = dsc("z", [S, 768])
        self.xbc = dsc("xbc", [1280, 4 + S])
        self.dt = dsc("dt", [S, 12])
        self.gate = [dsc(f"gate{b}", [2048, S], BF16) for b in range(4)]
        self.oA = dsc("oA", [3, S, 780])
        self.brT = dsc("brT", [4, 768, S], BF16)

    def V(self, fn, r, w):
        return self.fw.op("dve", fn, reads=r, writes=w)

    def A(self, fn, r, w):
        return self.fw.op("act", fn, reads=r, writes=w)

    def G(self, fn, r, w):
        return self.fw.op("pool", fn, reads=r, writes=w)

    def PE(self, fn, r, w, track=True):
        return self.fw.op("pe", fn, reads=r, writes=w, track=track)

    def psb(self, i, n):
        return self.ps[i][:, :].bitcast(BF16)[:, 0:n]

    def consts(self):
        fw = self.fw
        self.ps = [fw.psp(f"psb{i}", [128, 512], F32) for i in range(8)]
        self.psn = 0
        self.rot = list(range(8))
        idf = self.idf = fw.sbp("idf", [128, 128], F32)
        self.G(lambda e: e.memset(idf[:], 0.0), [], [idf])
        self.G(lambda e: e.affine_select(idf[:], idf[:], pattern=[[-1, 128]], compare_op=ALU.not_equal,
                                         fill=1.0, base=0, channel_multiplier=1), [idf], [idf])
        idb = self.idb = fw.sbp("idb", [128, 128], BF16)
        self.V(lambda e: e.tensor_copy(idb[:], idf[:]), [idf], [idb])
        tri = self.tri = fw.sbp("tri", [128, 128], F32)
        self.G(lambda e: e.memset(tri[:], 1.0), [], [tri])
        self.G(lambda e: e.affine_select(tri[:], tri[:], pattern=[[1, 128]], compare_op=ALU.is_ge,
                                         fill=0.0, base=0, channel_multiplier=-1), [tri], [tri])
        mgt = self.mgt = fw.sbp("mgt", [128, 128], F32)
        self.G(lambda e: e.memset(mgt[:], 1.0), [], [mgt])
        self.G(lambda e: e.affine_select(mgt[:], mgt[:], pattern=[[-1, 128]], compare_op=ALU.is_ge,
                                         fill=0.0, base=-1, channel_multiplier=1), [mgt], [mgt])
        ones = self.ones = fw.sbp("ones", [128, 128], F32)
        self.G(lambda e: e.memset(ones[:], 1.0), [], [ones])
        self.dmy = fw.sbp("dmy", [128, 4], F32)
        self.zer = fw.sbp("zer", [128, 8], F32)
        self.G(lambda e: e.memset(self.zer[:], 0.0), [], [self.zer])
        self.rowp = fw.sbp("rowp", [128, NR], F32)
        self.parp = fw.sbp("parp", [128, NP_], F32)
        self.gqs = fw.sbp("gqs", [128, 256], F32)
        self.lruc = fw.sbp("lruc", [128, 12], F32)
        self.aneg = fw.sbp("aneg", [128, 12], F32)

    def nps(self):
        self.psn = (self.psn + 1) % len(self.rot)
        return self.ps[self.rot[self.psn]]

    def load_params(self, l):
        fw = self.fw
        rowp, parp = self.rowp, self.parp
        fw.dma("sp", rowp[:], self.prow[l:l + 1, :].partition_broadcast(128), tw=rowp)
        fw.dma("sp", parp[:], self.ppar[l], tw=parp)
        gqs = self.gqs
        sc = float(128 ** -0.5)
        self.V(lambda e: e.tensor_scalar(gqs[:, 0:128], rowp[:, R_QA:R_QA + 128], sc, None, ALU.mult), [rowp], [gqs])
        self.V(lambda e: e.tensor_scalar(gqs[:, 128:256], rowp[:, R_QC:R_QC + 128], sc, None, ALU.mult), [rowp], [gqs])
        lruc = self.lruc
        fw.begin()
        y = fw.sb("spy", [128, 6]); ab = fw.sb("spa", [128, 6]); ex = fw.sb("spe", [128, 6])
        lam = parp[:, P_LAM:P_LAM + 6]
        self.V(lambda e: e.tensor_scalar(y[:], lam, -1.0, None, ALU.mult), [parp], [y])
        self.A(lambda e: e.activation(ab[:], y[:], AF.Abs), [y], [ab])
        self.A(lambda e: e.activation(ex[:], ab[:], AF.Exp, scale=-1.0), [ab], [ex])
        self.A(lambda e: e.activation(ex[:], ex[:], AF.Ln, bias=1.0, scale=1.0), [ex], [ex])
        self.V(lambda e: e.tensor_scalar(y[:], y[:], 0.0, None, ALU.max), [y], [y])
        self.V(lambda e: e.tensor_tensor(y[:], y[:], ex[:], ALU.add), [y, ex], [y])
        self.V(lambda e: e.tensor_scalar(lruc[:, 0:6], y[:], -8.0, None, ALU.mult), [y], [lruc])
        self.V(lambda e: e.tensor_scalar(lruc[:, 6:12], y[:], -16.0, None, ALU.mult), [y], [lruc])
        aneg = self.aneg
        self.A(lambda e: e.activation(aneg[:], rowp[:, R_ALOG:R_ALOG + 12], AF.Exp), [rowp], [aneg])
        self.V(lambda e: e.tensor_scalar(aneg[:], aneg[:], -1.0, None, ALU.mult), [aneg], [aneg])
        fw.end()

    def prepass(self):
        fw, S, L = self.fw, self.S, self.L
        dmy = self.dmy
        for l in range(L):
            for r0 in range(0, D, 256):
                fw.dma("pool", self.wb_in[l][r0:r0 + 256, :], self.w_in[l, r0:r0 + 256, :], tr=dmy)
                fw.dma("pool", self.wb_f1[l][r0:r0 + 256, :], self.w_f1[l, r0:r0 + 256, :], tr=dmy)
            for r0 in range(0, D, 1024):
                fw.dma("pool", self.wb_out[l][r0:r0 + 1024, :], self.w_out[l, r0:r0 + 1024, :], tr=dmy)
            for b in range(4):
                fw.dma("pool", self.wb_br[l][b], self.w_br[l, b], tr=dmy)
            for r0 in range(0, FF, 1408):
                fw.dma("pool", self.wb_f2[l][r0:r0 + 1408, :], self.w_f2[l, r0:r0 + 1408, :], tr=dmy)
        for r0 in range(0, S, 1024):
            fw.dma("sp", self.xres[r0:r0 + 1024, :], self.x_in[r0:r0 + 1024, :], tr=dmy)
        zer = self.zer
        for cb in range(6):
            fw.dma("sp", self.lrux[cb * 128:(cb + 1) * 128, 0:4], zer[:, 0:4], tr=zer)
        for cb in range(10):
            fw.dma("sp", self.xbc[cb * 128:(cb + 1) * 128, 0:4], zer[:, 0:4], tr=zer)
        fw.barrier()

    def build_hT(self, hT, t0, n_sub, grow, xr, hb, sm):
        fw = self.fw
        rowp = self.rowp
        for s in range(n_sub):
            xt = xr[s % len(xr)]
            fw.dma("sp", xt[:], self.xres[t0 + s * 128:t0 + (s + 1) * 128, :], tw=xt)
            h = hb[s]
            ss = sm[s % len(sm)]
            self.A(lambda e: e.activation(h[:], xt[:], AF.Square, accum_out=ss[:, 0:1]), [xt], [h, ss])
            self.A(lambda e: e.activation(ss[:, 1:2], ss[:, 0:1], AF.Sqrt, bias=1e-6, scale=1.0 / D), [ss], [ss])
            self.V(lambda e: e.reciprocal(ss[:, 1:2], ss[:, 1:2]), [ss], [ss])
            self.V(lambda e: e.scalar_tensor_tensor(h[:], xt[:], ss[:, 1:2], rowp[:, grow:grow + D],
                                                    ALU.mult, ALU.mult), [xt, ss, rowp], [h])
        for kc in range(16):
            for g0 in range(0, n_sub, 4):
                ng = min(4, n_sub - g0)
                p = self.nps()
                pv = p[:, :].bitcast(BF16)
                for s in range(g0, g0 + ng):
                    self.PE(lambda e: e.transpose(pv[:, (s - g0) * 128:(s - g0 + 1) * 128],
                                                  hb[s][:, kc * 128:(kc + 1) * 128], self.idb[:]),
                            [hb[s], self.idb], [p])
                dst = hT[:, kc, g0 * 128:(g0 + ng) * 128]
                if kc % 2 == 0:
                    self.A(lambda e: e.copy(dst, pv[:, 0:ng * 128]), [p], [hT])
                else:
                    self.V(lambda e: e.tensor_copy(dst, pv[:, 0:ng * 128]), [p], [hT])

    def phase_proj(self, l):
        fw, S = self.fw, self.S
        rowp = self.rowp
        fw.begin()
        TT = 512
        hTs = fw.ring("hT", 2, [128, 16, TT], BF16)
        wr = fw.ring("wr", 3, [128, 16, 512], BF16)
        xr = fw.ring("xr", 2, [128, D], F32)
        hb = fw.ring("hb", 4, [128, D], BF16)
        sm = fw.ring("sm", 2, [128, 2], F32)
        sq = fw.sb("sq", [128, 512], F32)
        tmp = fw.ring("tmp", 2, [128, 512], F32)
        obf = fw.ring("obf", 3, [128, 512], BF16)
        of32 = fw.ring("of32", 3, [128, 512], F32)
        ssr = fw.ring("ssr", 2, [128, 8], F32)
        dts = fw.ring("dts", 2, [128, 48], F32)
        tiles = []
        for i in range(6):
            tiles.append((C_QA + i * 384, 384, "qk", (self.qa, i * 384, self.gqs, 0)))
        for i in range(6):
            tiles.append((C_KA + i * 384, 384, "qk", (self.ka, i * 384, rowp, R_KA)))
        for i in range(6):
            tiles.append((C_VA + i * 384, 384, "cbf", (self.va, i * 384)))
        for i in range(2):
            tiles.append((C_QC + i * 384, 384, "qk", (self.qc, i * 384, self.gqs, 128)))
        tiles.append((C_KC, 256, "qk", (self.kc, 0, rowp, R_KC)))
        tiles.append((C_VC, 256, "cbf", (self.vc, 0)))
        for i in range(2):
            tiles.append((C_LX + i * 384, 384, "fm", ("lrux", i * 384)))
        for i in range(2):
            tiles.append((C_LG + i * 384, 384, "fm", ("lrug", i * 384)))
        for i in range(2):
            tiles.append((C_Z + i * 384, 384, "cf32", (self.z, i * 384)))
        tiles.append((C_XBC, 512, "fm", ("xbc", 0)))
        tiles.append((C_XBC + 512, 512, "fm", ("xbc", 512)))
        tiles.append((C_XBC + 1024, 256, "fm", ("xbc", 1024)))
        tiles.append((C_DT, 12, "dt", None))
        for i in range(16):
            tiles.append((C_G + i * 512, 512, "fm", ("gate", i * 512)))
        wi = 0
        oi = 0
        for tt in range(S // TT):
            t0 = tt * TT
            hT = hTs[tt % 2]
            self.build_hT(hT, t0, 4, R_MIX, xr, hb, sm)
            for (c0, ncw, kind, arg) in tiles:
                wt = wr[wi % 3]
                wi += 1
                fw.dma("sp", wt[:, :, 0:ncw],
                       self.wb_in[l][:, c0:c0 + ncw].rearrange("(kc p) c -> p kc c", p=128), tw=wt)
                if kind == "fm":
                    name, d0 = arg
                    for cc in range(ncw // 128):
                        p = self.nps()
                        for kc in range(16):
                            self.PE(lambda e: e.matmul(p[:, :], wt[:, kc, cc * 128:(cc + 1) * 128], hT[:, kc, :],
                                                       start=(kc == 0), stop=(kc == 15)),
                                    [wt, hT], [p], track=(kc == 15))
                        oi += 1
                        ch0 = d0 + cc * 128
                        if name == "gate":
                            o = obf[oi % 3]
                            self.A(lambda e: e.activation(o[:], p[:, :], AF.Sigmoid), [p], [o])
                            fw.dma("pool", self.gate[ch0 // 2048][ch0 % 2048:ch0 % 2048 + 128, t0:t0 + TT], o[:], tr=o)
                        elif name == "lrug":
                            o = of32[oi % 3]
                            self.A(lambda e: e.activation(o[:], p[:, :], AF.Gelu_apprx_tanh), [p], [o])
                            fw.dma("pool", self.lrug[ch0:ch0 + 128, t0:t0 + TT], o[:], tr=o)
                        else:
                            o = of32[oi % 3]
                            self.V(lambda e: e.tensor_copy(o[:], p[:, :]), [p], [o])
                            dst = self.lrux if name == "lrux" else self.xbc
                            fw.dma("pool", dst[ch0:ch0 + 128, 4 + t0:4 + t0 + TT], o[:], tr=o)
                    continue
                for s in range(4):
                    p = self.nps()
                    for kc in range(16):
                        self.PE(lambda e: e.matmul(p[:, 0:ncw], hT[:, kc, s * 128:(s + 1) * 128], wt[:, kc, 0:ncw],
                                                   start=(kc == 0), stop=(kc == 15)),
                                [wt, hT], [p], track=(kc == 15))
                    oi += 1
                    r0 = t0 + s * 128
                    if kind == "qk":
                        dst, d0, gt, goff = arg
                        nh = ncw // 128
                        ss = ssr[oi % 2]
                        tm = tmp[oi % 2]
                        o = obf[oi % 3]
                        self.A(lambda e: e.activation(sq[:, 0:ncw], p[:, 0:ncw], AF.Square), [p], [sq])
                        self.V(lambda e: e.tensor_reduce(ss[:, 0:nh], sq[:, 0:ncw].rearrange("p (h d) -> p h d", h=nh),
                                                         AX.X, ALU.add), [sq], [ss])
                        self.A(lambda e: e.activation(ss[:, 4:4 + nh], ss[:, 0:nh], AF.Sqrt, bias=1e-6, scale=1.0 / 128),
                               [ss], [ss])
                        self.V(lambda e: e.reciprocal(ss[:, 4:4 + nh], ss[:, 4:4 + nh]), [ss], [ss])
                        self.V(lambda e: e.tensor_tensor(tm[:, 0:ncw].rearrange("p (h d) -> p h d", h=nh),
                                                         p[:, 0:ncw].rearrange("p (h d) -> p h d", h=nh),
                                                         ss[:, 4:4 + nh].unsqueeze(2).to_broadcast([128, nh, 128]),
                                                         ALU.mult), [p, ss], [tm])
                        self.G(lambda e: e.tensor_tensor(o[:, 0:ncw].rearrange("p (h d) -> p h d", h=nh),
                                                         tm[:, 0:ncw].rearrange("p (h d) -> p h d", h=nh),
                                                         gt[:, goff:goff + 128].unsqueeze(1).to_broadcast([128, nh, 128]),
                                                         ALU.mult), [tm, gt], [o])
                        fw.dma("pool", dst[r0:r0 + 128, d0:d0 + ncw], o[:, 0:ncw], tr=o)
                    elif kind == "cbf":
                        dst, d0 = arg
                        o = obf[oi % 3]
                        self.A(lambda e: e.copy(o[:, 0:ncw], p[:, 0:ncw]), [p], [o])
                        fw.dma("pool", dst[r0:r0 + 128, d0:d0 + ncw], o[:, 0:ncw], tr=o)
                    elif kind == "cf32":
                        dst, d0 = arg
                        o = of32[oi % 3]
                        self.V(lambda e: e.tensor_copy(o[:, 0:ncw], p[:, 0:ncw]), [p], [o])
                        fw.dma("pool", dst[r0:r0 + 128, d0:d0 + ncw], o[:, 0:ncw], tr=o)
                    elif kind == "dt":
                        d = dts[oi % 2]
                        x_, ab, ex, ou = d[:, 0:12], d[:, 12:24], d[:, 24:36], d[:, 36:48]
                        self.V(lambda e: e.tensor_tensor(x_, p[:, 0:12], rowp[:, R_DTB:R_DTB + 12], ALU.add), [p, rowp], [d])
                        self.A(lambda e: e.activation(ab, x_, AF.Abs), [d], [d])
                        self.A(lambda e: e.activation(ex, ab, AF.Exp, scale=-1.0), [d], [d])
                        self.A(lambda e: e.activation(ex, ex, AF.Ln, bias=1.0, scale=1.0), [d], [d])
                        self.V(lambda e: e.scalar_tensor_tensor(ou, x_, 0.0, ex, ALU.max, ALU.add), [d], [d])
                        fw.dma("pool", self.dt[r0:r0 + 128, :], ou, tr=d)
        fw.end()

    def store_fm(self, otok, br, tok0, stg):
        fw = self.fw
        for h in range(6):
            p = self.nps()
            pv = p[:, :].bitcast(BF16)
            for j in range(4):
                self.PE(lambda e: e.transpose(pv[:, j * 128:(j + 1) * 128], otok[j][:, h * 128:(h + 1) * 128], self.idb[:]),
                        [otok[j], self.idb], [p])
            if h % 2 == 0:
                self.A(lambda e: e.copy(stg[:, h, :], pv[:, 0:512]), [p], [stg])
            else:
                self.V(lambda e: e.tensor_copy(stg[:, h, :], pv[:, 0:512]), [p], [stg])
        fw.dma("pool", self.brT[br, :, tok0:tok0 + 512].rearrange("(h p) t -> p h t", p=128), stg[:], tr=stg)

    def attention(self, qd, kd, vd, nkv, dil, tabbase, mode, gi):
        fw, S = self.fw, self.S
        rowp = self.rowp
        fw.begin()
        btab = fw.sb("btab", [128, 6, 256], F32)
        fw.dma("sp", btab[:], self.btab[:, tabbase:tabbase + 6, :], tw=btab)
        qr = fw.ring("qr", 2, [128, 768], BF16)
        kr = fw.ring("kr", 2, [128, nkv * 128], BF16)
        vr = fw.ring("vr", 3, [128, nkv * 128], BF16)
        kT = fw.ring("kT", 3, [128, nkv, 128], BF16)
        qT = fw.ring("qT", 2, [128, 128], BF16)
        ssb = fw.ring("ssb", 2, [128, 256], F32)
        pb = fw.ring("pb", 2, [128, 256], BF16)
        pT = fw.ring("pT", 2, [128, 2, 128], BF16)
        ml = fw.ring("ml", 2, [128, 4], F32)
        if mode == "A":
            ost = fw.ring("ost", 2, [128, 768], F32)
            mls = fw.ring("mls", 2, [128, 12], F32)
        else:
            otok = fw.ring("otok", 8, [128, 768], BF16)
            stg = fw.ring("stg", 2, [128, 6, 512], BF16)
        nblk = S // dil // 128
        it = 0
        hi = 0
        for r in range(dil):
            for b in range(nblk):
                row0 = r + dil * 128 * b
                rows = slice(row0, row0 + dil * 127 + 1, dil)
                q = qr[it % 2]; k = kr[it % 2]; v = vr[it % 3]; kTc = kT[it % 3]
                vp = vr[(it - 1) % 3]; kTp = kT[(it - 1) % 3]
                fw.dma("sp", q[:], qd[rows, :], tw=q)
                fw.dma("sp", k[:], kd[rows, :], tw=k)
                fw.dma("sp", v[:], vd[rows, :], tw=v)
                for hk in range(nkv):
                    p = self.nps()
                    pv = p[:, :].bitcast(BF16)
                    self.PE(lambda e: e.transpose(pv[:, 0:128], k[:, hk * 128:(hk + 1) * 128], self.idb[:]), [k, self.idb], [p])
                    self.A(lambda e: e.copy(kTc[:, hk, :], pv[:, 0:128]), [p], [kTc])
                if mode == "A":
                    os_ = ost[it % 2]; ms = mls[it % 2]
                else:
                    ot = otok[it % 8]
                lo = 128 if b == 0 else 0
                for h in range(6):
                    hk = h if nkv == 6 else h // 3
                    hi += 1
                    p = self.nps()
                    pv = p[:, :].bitcast(BF16)
                    qt = qT[hi % 2]
                    self.PE(lambda e: e.transpose(pv[:, 0:128], q[:, h * 128:(h + 1) * 128], self.idb[:]), [q, self.idb], [p])
                    self.V(lambda e: e.tensor_copy(qt[:], pv[:, 0:128]), [p], [qt])
                    sp_ = self.nps()
                    if b > 0:
                        self.PE(lambda e: e.matmul(sp_[:, 0:128], qt[:], kTp[:, hk, :], start=True, stop=True), [qt, kTp], [sp_])
                    self.PE(lambda e: e.matmul(sp_[:, 128:256], qt[:], kTc[:, hk, :], start=True, stop=True), [qt, kTc], [sp_])
                    sb_ = ssb[hi % 2]; m = ml[hi % 2]; pp = pb[hi % 2]; pt = pT[hi % 2]
                    self.V(lambda e: e.tensor_tensor(sb_[:, lo:256], sp_[:, lo:256], btab[:, h, lo:256], ALU.add), [sp_, btab], [sb_])
                    self.V(lambda e: e.reduce_max(m[:, 0:1], sb_[:, lo:256], AX.X), [sb_], [m])
                    if mode == "C":
                        self.V(lambda e: e.tensor_tensor(m[:, 0:1], m[:, 0:1], rowp[:, R_SINK + h:R_SINK + h + 1], ALU.max), [m, rowp], [m])
                    self.V(lambda e: e.tensor_scalar(m[:, 1:2], m[:, 0:1], -1.0, None, ALU.mult), [m], [m])
                    self.A(lambda e: e.activation(pp[:, lo:256], sb_[:, lo:256], AF.Exp, bias=m[:, 1:2], scale=1.0,
                                                  accum_out=m[:, 2:3]), [sb_, m], [pp, m])
                    if mode == "C":
                        self.A(lambda e: e.activation(m[:, 3:4], rowp[:, R_SINK + h:R_SINK + h + 1], AF.Exp, bias=m[:, 1:2], scale=1.0), [rowp, m], [m])
                        self.V(lambda e: e.tensor_tensor(m[:, 2:3], m[:, 2:3], m[:, 3:4], ALU.add), [m], [m])
                        self.V(lambda e: e.reciprocal(m[:, 2:3], m[:, 2:3]), [m], [m])
                    tp = self.nps()
                    tv = tp[:, :].bitcast(BF16)
                    if b > 0:
                        self.PE(lambda e: e.transpose(tv[:, 0:128], pp[:, 0:128], self.idb[:]), [pp, self.idb], [tp])
                    self.PE(lambda e: e.transpose(tv[:, 128:256], pp[:, 128:256], self.idb[:]), [pp, self.idb], [tp])
                    self.A(lambda e: e.copy(pt[:, :, :].rearrange("p a b -> p (a b)")[:, lo:256], tv[:, lo:256]), [tp], [pt])
                    op_ = self.nps()
                    if b > 0:
                        self.PE(lambda e: e.matmul(op_[:, 0:128], pt[:, 0, :], vp[:, hk * 128:(hk + 1) * 128], start=True, stop=False), [pt, vp], [op_], track=False)
                    self.PE(lambda e: e.matmul(op_[:, 0:128], pt[:, 1, :], v[:, hk * 128:(hk + 1) * 128], start=(b == 0), stop=True), [pt, v], [op_])
                    if mode == "A":
                        self.V(lambda e: e.tensor_copy(os_[:, h * 128:(h + 1) * 128], op_[:, 0:128]), [op_], [os_])
                        self.G(lambda e: e.tensor_copy(ms[:, h:h + 1], m[:, 0:1]), [m], [ms])
                        self.G(lambda e: e.tensor_copy(ms[:, 6 + h:7 + h], m[:, 2:3]), [m], [ms])
                    else:
                        self.V(lambda e: e.tensor_scalar(ot[:, h * 128:(h + 1) * 128], op_[:, 0:128], m[:, 2:3], None, ALU.mult), [op_, m], [ot])
                if mode == "A":
                    fw.dma("pool", self.oA[gi, rows, 0:768], os_[:], tr=os_)
                    fw.dma("pool", self.oA[gi, rows, 768:780], ms[:], tr=ms)
                elif b % 4 == 3:
                    self.store_fm([otok[(it - 3 + j) % 8] for j in range(4)], 2, (b - 3) * 128, stg[(b // 4) % 2])
                it += 1
        fw.end()

    def merge_A(self):
        fw, S = self.fw, self.S
        fw.begin()
        og = [fw.ring(f"og{g}", 2, [128, 780], F32) for g in range(3)]
        sm = fw.ring("msm", 2, [128, 64], F32)
        acc = fw.ring("macc", 2, [128, 768], F32)
        tm = fw.ring("mtm", 2, [128, 768], F32)
        otok = fw.ring("motok", 8, [128, 768], BF16)
        stg = fw.ring("mstg", 2, [128, 6, 512], BF16)
        for i in range(S // 128):
            o = [og[g][i % 2] for g in range(3)]
            for g in range(3):
                fw.dma("sp", o[g][:], self.oA[g, i * 128:(i + 1) * 128, :], tw=o[g])
            s_ = sm[i % 2]
            M, e_, w_, dn = s_[:, 0:6], s_[:, 8:26], s_[:, 26:44], s_[:, 44:50]
            self.V(lambda e: e.tensor_tensor(M, o[0][:, 768:774], o[1][:, 768:774], ALU.max), [o[0], o[1]], [s_])
            self.V(lambda e: e.tensor_tensor(M, M, o[2][:, 768:774], ALU.max), [s_, o[2]], [s_])
            for g in range(3):
                self.V(lambda e: e.tensor_tensor(e_[:, g * 6:(g + 1) * 6], o[g][:, 768:774], M, ALU.subtract), [o[g], s_], [s_])
            self.A(lambda e: e.activation(e_, e_, AF.Exp), [s_], [s_])
            for g in range(3):
                self.V(lambda e: e.tensor_tensor(w_[:, g * 6:(g + 1) * 6], e_[:, g * 6:(g + 1) * 6], o[g][:, 774:780], ALU.mult), [o[g], s_], [s_])
            self.V(lambda e: e.tensor_tensor(dn, w_[:, 0:6], w_[:, 6:12], ALU.add), [s_], [s_])
            self.V(lambda e: e.tensor_tensor(dn, dn, w_[:, 12:18], ALU.add), [s_], [s_])
            self.V(lambda e: e.reciprocal(dn, dn), [s_], [s_])
            for g in range(3):
                self.V(lambda e: e.tensor_tensor(e_[:, g * 6:(g + 1) * 6], e_[:, g * 6:(g + 1) * 6], dn, ALU.mult), [s_], [s_])
            a_ = acc[i % 2]; t_ = tm[i % 2]; ot = otok[i % 8]

            def bc(g):
                return e_[:, g * 6:(g + 1) * 6].unsqueeze(2).to_broadcast([128, 6, 128])

            def v3(t, n=768):
                return t[:, 0:768].rearrange("p (h d) -> p h d", h=6)
            self.V(lambda e: e.tensor_tensor(v3(a_), v3(o[0]), bc(0), ALU.mult), [o[0], s_], [a_])
            self.G(lambda e: e.tensor_tensor(v3(t_), v3(o[1]), bc(1), ALU.mult), [o[1], s_], [t_])
            self.V(lambda e: e.tensor_tensor(a_[:], a_[:], t_[:], ALU.add), [a_, t_], [a_])
            self.G(lambda e: e.tensor_tensor(v3(t_), v3(o[2]), bc(2), ALU.mult), [o[2], s_], [t_])
            self.V(lambda e: e.tensor_tensor(ot[:], a_[:], t_[:], ALU.add), [a_, t_], [ot])
            if i % 4 == 3:
                self.store_fm([otok[(i - 3 + j) % 8] for j in range(4)], 0, (i - 3) * 128, stg[(i // 4) % 2])
        fw.end()

    def lru(self, l):
        fw, S = self.fw, self.S
        parp = self.parp
        fw.begin()
        TL = 1024
        gw = fw.sb("gw", [128, 12, 128], F32)
        fw.dma("sp", gw[:], self.lgw[l].rearrange("g h i o -> i (g h) o"), tw=gw)
        xin = fw.ring("lx", 2, [128, 3 + TL], F32)
        xc = fw.ring("lxc", 2, [128, TL], F32)
        rg = fw.ring("lr", 2, [128, TL], F32)
        ig = fw.ring("li", 2, [128, TL], F32)
        aa = fw.ring("la", 2, [128, TL], F32)
        uu = fw.ring("lu", 2, [128, TL], F32)
        hh = fw.ring("lh", 2, [128, TL], F32)
        gg = fw.ring("lg", 2, [128, TL], F32)
        ob = fw.ring("lob", 2, [128, TL], BF16)
        it = 0
        for cb in range(6):
            for tt in range(S // TL):
                t0 = tt * TL
                x = xin[it % 2]; c = xc[it % 2]; r_ = rg[it % 2]; i_ = ig[it % 2]
                a = aa[it % 2]; u = uu[it % 2]; h = hh[it % 2]; g = gg[it % 2]; o = ob[it % 2]
                hprev = hh[(it - 1) % 2]
                fw.dma("sp", x[:], self.lrux[cb * 128:(cb + 1) * 128, 1 + t0:4 + t0 + TL], tw=x)
                fw.dma("sp", g[:], self.lrug[cb * 128:(cb + 1) * 128, t0:t0 + TL], tw=g)
                w = lambda k: parp[:, P_LCW + cb * 4 + k:P_LCW + cb * 4 + k + 1]
                self.V(lambda e: e.tensor_scalar(c[:], x[:, 0:TL], w(0), parp[:, P_LCB + cb:P_LCB + cb + 1], ALU.mult, ALU.add), [x, parp], [c])
                for k in range(1, 4):
                    self.V(lambda e: e.scalar_tensor_tensor(c[:], x[:, k:k + TL], w(k), c[:], ALU.mult, ALU.add), [x, parp, c], [c])
                for j in range(TL // 512):
                    js = slice(j * 512, (j + 1) * 512)
                    p1 = self.nps()
                    self.PE(lambda e: e.matmul(p1[:, :], gw[:, cb, :], c[:, js], start=True, stop=True), [gw, c], [p1])
                    self.A(lambda e: e.activation(r_[:, js], p1[:, :], AF.Sigmoid, bias=parp[:, P_LGB + cb:P_LGB + cb + 1], scale=1.0), [p1, parp], [r_])
                    p2 = self.nps()
                    self.PE(lambda e: e.matmul(p2[:, :], gw[:, 6 + cb, :], c[:, js], start=True, stop=True), [gw, c], [p2])
                    self.A(lambda e: e.activation(i_[:, js], p2[:, :], AF.Sigmoid, bias=parp[:, P_LGB + 6 + cb:P_LGB + 7 + cb], scale=1.0), [p2, parp], [i_])
                self.A(lambda e: e.activation(a[:], r_[:], AF.Exp, scale=self.lruc[:, cb:cb + 1]), [r_, self.lruc], [a])
                self.A(lambda e: e.activation(u[:], r_[:], AF.Exp, scale=self.lruc[:, 6 + cb:7 + cb]), [r_, self.lruc], [u])
                self.A(lambda e: e.activation(u[:], u[:], AF.Sqrt, bias=1.0, scale=-1.0), [u], [u])
                self.G(lambda e: e.tensor_tensor(i_[:], i_[:], c[:], ALU.mult), [i_, c], [i_])
                self.V(lambda e: e.tensor_tensor(u[:], u[:], i_[:], ALU.mult), [u, i_], [u])
                if tt == 0:
                    self.V(lambda e: e.tensor_tensor_scan(h[:], a[:], u[:], 0.0, ALU.mult, ALU.add), [a, u], [h])
                else:
                    self.V(lambda e: e.tensor_tensor_scan(h[:], a[:], u[:], hprev[:, TL - 1:TL], ALU.mult, ALU.add), [a, u, hprev], [h])
                self.G(lambda e: e.tensor_tensor(o[:], h[:], g[:], ALU.mult), [h, g], [o])
                fw.dma("pool", self.brT[1, cb * 128:(cb + 1) * 128, t0:t0 + TL], o[:], tr=o)
                it += 1
        fw.end()

    def ssd(self, l):
        fw, S = self.fw, self.S
        rowp, parp = self.rowp, self.parp
        fw.begin()
        self.rot = list(range(6))
        xin = fw.ring("sx", 2, [128, 515], F32)
        cv = fw.ring("scv", 2, [128, 512], F32)
        xbs = [fw.sb(f"xbs{i}", [128, 512], F32) for i in range(6)]
        bcf = [fw.sb(f"bcf{i}", [128, 512], BF16) for i in range(4)]
        bf32 = [fw.sb(f"bf32{i}", [128, 512], F32) for i in range(2)]
        H = fw.sb("H", [128, 768], F32)
        Hb = fw.sb("Hb", [128, 768], BF16)
        self.G(lambda e: e.memset(H[:], 0.0), [], [H])
        self.G(lambda e: e.memset(Hb[:], 0.0), [], [Hb])
        xtok = fw.ring("xtok", 2, [128, 768], F32)
        btok = fw.ring("btok", 2, [128, 256], BF16)
        dtt = fw.ring("dtt", 2, [128, 96], F32)
        zt = fw.ring("zt", 2, [128, 768], F32)
        xdt = fw.ring("xdt", 2, [128, 768], BF16)
        xw = fw.ring("xw", 2, [128, 768], BF16)
        Le = fw.ring("Le", 2, [128, 128], F32)
        dec = fw.ring("dec", 2, [128, 128], F32)
        Mt = fw.ring("Mt", 3, [128, 128], BF16)
        cbm = fw.ring("cbm", 2, [128, 2, 128], F32)
        y = fw.ring("y", 2, [128, 768], F32)
        t1 = fw.ring("t1", 2, [128, 768], F32)
        nrm = fw.ring("nrm", 2, [128, 8], F32)
        otok = fw.ring("sotok", 8, [128, 768], BF16)
        stg = fw.ring("sstg", 2, [128, 6, 512], BF16)
        it = 0
        for sc in range(S // 512):
            t0 = sc * 512
            for cb in range(10):
                x = xin[cb % 2]; c = cv[cb % 2]
                fw.dma("sp", x[:], self.xbc[cb * 128:(cb + 1) * 128, 1 + t0:4 + t0 + 512], tw=x)
                w = lambda k: parp[:, P_SCW + cb * 4 + k:P_SCW + cb * 4 + k + 1]
                self.V(lambda e: e.tensor_scalar(c[:], x[:, 0:512], w(0), parp[:, P_SCB + cb:P_SCB + cb + 1], ALU.mult, ALU.add), [x, parp], [c])
                for k in range(1, 4):
                    self.V(lambda e: e.scalar_tensor_tensor(c[:], x[:, k:k + 512], w(k), c[:], ALU.mult, ALU.add), [x, parp, c], [c])
                if cb < 6:
                    self.A(lambda e: e.activation(xbs[cb][:], c[:], AF.Silu), [c], [xbs[cb]])
                elif cb < 8:
                    self.A(lambda e: e.activation(bf32[cb - 6][:], c[:], AF.Silu), [c], [bf32[cb - 6]])
                    self.G(lambda e: e.tensor_copy(bcf[cb - 6][:], bf32[cb - 6][:]), [bf32[cb - 6]], [bcf[cb - 6]])
                else:
                    self.A(lambda e: e.activation(bcf[cb - 6][:], c[:], AF.Silu), [c], [bcf[cb - 6]])
            for ci in range(4):
                cs = slice(ci * 128, (ci + 1) * 128)
                tok0 = t0 + ci * 128
                xt = xtok[it % 2]; bt = btok[it % 2]; d = dtt[it % 2]; z = zt[it % 2]
                fw.dma("sp", d[:, 0:12], self.dt[tok0:tok0 + 128, :], tw=d)
                fw.dma("sp", z[:], self.z[tok0:tok0 + 128, :], tw=z)
                for half in range(2):
                    p = self.nps()
                    for j in range(3):
                        self.PE(lambda e: e.transpose(p[:, j * 128:(j + 1) * 128], xbs[half * 3 + j][:, cs], self.idf[:]), [xbs[half * 3 + j], self.idf], [p])
                    self.A(lambda e: e.copy(xt[:, half * 384:(half + 1) * 384], p[:, 0:384]), [p], [xt])
                p = self.nps()
                for g in range(2):
                    self.PE(lambda e: e.transpose(p[:, g * 128:(g + 1) * 128], bf32[g][:, cs], self.idf[:]), [bf32[g], self.idf], [p])
                self.V(lambda e: e.tensor_copy(bt[:], p[:, 0:256]), [p], [bt])
                dtv, dtA, acs, dte, ea, cd = d[:, 0:12], d[:, 12:24], d[:, 24:36], d[:, 36:48], d[:, 48:60], d[:, 60:72]
                self.V(lambda e: e.tensor_tensor(dtA, dtv, self.aneg[:], ALU.mult), [d, self.aneg], [d])
                pc = self.nps()
                self.PE(lambda e: e.matmul(pc[:, 0:12], self.tri[:], dtA, start=True, stop=True), [self.tri, d], [pc])
                self.PE(lambda e: e.matmul(pc[:, 16:28], self.ones[:], dtA, start=True, stop=True), [self.ones, d], [pc])
                self.V(lambda e: e.tensor_copy(acs, pc[:, 0:12]), [pc], [d])
                self.A(lambda e: e.activation(ea, pc[:, 0:12], AF.Exp), [pc], [d])
                self.A(lambda e: e.activation(cd, pc[:, 16:28], AF.Exp), [pc], [d])
                self.V(lambda e: e.tensor_tensor(dte, pc[:, 16:28], acs, ALU.subtract), [pc, d], [d])
                self.A(lambda e: e.activation(dte, dte, AF.Exp), [d], [d])
                xd = xdt[it % 2]; xw_ = xw[it % 2]

                def v12(t):
                    return t[:, 0:768].rearrange("p (h d) -> p h d", h=12)

                def b12(a):
                    return a.unsqueeze(2).to_broadcast([128, 12, 64])
                tt1 = t1[it % 2]
                self.V(lambda e: e.tensor_tensor(v12(tt1), v12(xt), b12(dtv), ALU.mult), [xt, d], [tt1])
                self.G(lambda e: e.tensor_copy(xd[:], tt1[:]), [tt1], [xd])
                self.V(lambda e: e.tensor_tensor(v12(xw_), v12(tt1), b12(dte), ALU.mult), [tt1, d], [xw_])
                cm = cbm[it % 2]
                for g in range(2):
                    pcb = self.nps()
                    self.PE(lambda e: e.matmul(pcb[:, 0:128], bcf[g][:, cs], bcf[2 + g][:, cs], start=True, stop=True), [bcf[g], bcf[2 + g]], [pcb])
                    self.V(lambda e: e.tensor_tensor(cm[:, g, :], pcb[:, 0:128], self.tri[:], ALU.mult), [pcb, self.tri], [cm])
                yps = [self.ps[6], self.ps[7]]
                for e_ in range(12):
                    g = e_ // 6
                    L_ = Le[e_ % 2]; dc = dec[e_ % 2]; M_ = Mt[e_ % 3]
                    self.G(lambda e: e.tensor_scalar(L_[:], self.mgt[:], dtA[:, e_:e_ + 1], None, ALU.mult), [self.mgt, d], [L_])
                    psg = self.nps()
                    self.PE(lambda e: e.matmul(psg[:, 0:128], L_[:], self.tri[:], start=True, stop=True), [L_, self.tri], [psg])
                    self.A(lambda e: e.activation(dc[:], psg[:, 0:128], AF.Exp), [psg], [dc])
                    self.V(lambda e: e.tensor_tensor(M_[:], dc[:], cm[:, g, :], ALU.mult), [dc, cm], [M_])
                    yp = yps[g]
                    eo = (e_ % 6) * 64
                    self.PE(lambda e: e.matmul(yp[:, eo:eo + 64], M_[:], xd[:, e_ * 64:(e_ + 1) * 64], start=True, stop=True), [M_, xd], [yp])
                yy = y[it % 2]
                for g in range(2):
                    po = self.nps()
                    self.PE(lambda e: e.matmul(po[:, 0:384], bcf[2 + g][:, cs], Hb[:, g * 384:(g + 1) * 384], start=True, stop=True), [bcf[2 + g], Hb], [po])
                    gs = slice(g * 384, (g + 1) * 384)
                    self.V(lambda e: e.tensor_tensor(tt1[:, gs].rearrange("p (h d) -> p h d", h=6), po[:, 0:384].rearrange("p (h d) -> p h d", h=6),
                                                     ea[:, g * 6:(g + 1) * 6].unsqueeze(2).to_broadcast([128, 6, 64]), ALU.mult), [po, d], [tt1])
                    self.V(lambda e: e.tensor_tensor(yy[:, gs], tt1[:, gs], yps[g][:, 0:384], ALU.add), [tt1, yps[g]], [yy])
                for g in range(2):
                    pst = self.nps()
                    gs = slice(g * 384, (g + 1) * 384)
                    self.PE(lambda e: e.matmul(pst[:, 0:384], bt[:, g * 128:(g + 1) * 128], xw_[:, gs], start=True, stop=True), [bt, xw_], [pst])
                    self.V(lambda e: e.tensor_tensor(H[:, gs].rearrange("p (h d) -> p h d", h=6), H[:, gs].rearrange("p (h d) -> p h d", h=6),
                                                     cd[:, g * 6:(g + 1) * 6].unsqueeze(2).to_broadcast([128, 6, 64]), ALU.mult), [H, d], [H])
                    self.V(lambda e: e.tensor_tensor(H[:, gs], H[:, gs], pst[:, 0:384], ALU.add), [H, pst], [H])
                self.G(lambda e: e.tensor_copy(Hb[:], H[:]), [H], [Hb])
                self.G(lambda e: e.tensor_tensor(v12(tt1), v12(xt), b12(rowp[:, R_DSK:R_DSK + 12]), ALU.mult), [xt, rowp], [tt1])
                self.V(lambda e: e.tensor_tensor(yy[:], yy[:], tt1[:], ALU.add), [yy, tt1], [yy])
                self.A(lambda e: e.activation(z[:], z[:], AF.Silu), [z], [z])
                self.V(lambda e: e.tensor_tensor(yy[:], yy[:], z[:], ALU.mult), [yy, z], [yy])
                n_ = nrm[it % 2]
                for g in range(2):
                    gs = slice(g * 384, (g + 1) * 384)
                    self.A(lambda e: e.activation(tt1[:, gs], yy[:, gs], AF.Square, accum_out=n_[:, g:g + 1]), [yy], [tt1, n_])
                self.A(lambda e: e.activation(n_[:, 2:4], n_[:, 0:2], AF.Sqrt, bias=1e-5, scale=1.0 / 384), [n_], [n_])
                self.V(lambda e: e.reciprocal(n_[:, 2:4], n_[:, 2:4]), [n_], [n_])
                ot = otok[it % 8]
                self.V(lambda e: e.tensor_tensor(yy[:, 0:768].rearrange("p (g d) -> p g d", g=2), yy[:, 0:768].rearrange("p (g d) -> p g d", g=2),
                                                 n_[:, 2:4].unsqueeze(2).to_broadcast([128, 2, 384]), ALU.mult), [yy, n_], [yy])
                self.G(lambda e: e.tensor_tensor(ot[:], yy[:], rowp[:, R_SNORM:R_SNORM + 768], ALU.mult), [yy, rowp], [ot])
                if ci == 3:
                    self.store_fm([otok[(it - 3 + j) % 8] for j in range(4)], 3, t0, stg[sc % 2])
                it += 1
        fw.end()
        self.rot = list(range(8))

    def merge_out(self, l):
        fw, S = self.fw, self.S
        fw.begin()
        TT = 512
        brt = [fw.ring(f"brt{b}", 2, [128, 6, TT], BF16) for b in range(4)]
        wb = fw.ring("wbr", 3, [128, 6, 128], BF16)
        gt = fw.ring("gt", 3, [128, TT], BF16)
        acc = fw.ring("acc", 2, [128, TT], F32)
        tmp = fw.ring("mtmp", 2, [128, TT], F32)
        mT = fw.ring("mT", 2, [128, 16, TT], BF16)
        wo = fw.ring("wo", 2, [128, 16, 512], BF16)
        xr = fw.ring("xr", 2, [128, D], F32)
        wi = 0
        for tt in range(S // TT):
            t0 = tt * TT
            m_ = mT[tt % 2]
            for b in range(4):
                t = brt[b][tt % 2]
                fw.dma("sp", t[:], self.brT[b, :, t0:t0 + TT].rearrange("(kc p) t -> p kc t", p=128), tw=t)
            for cc in range(16):
                a_ = acc[cc % 2]
                for b in range(4):
                    w = wb[wi % 3]; g = gt[wi % 3]
                    wi += 1
                    fw.dma("sp", w[:], self.wb_br[l][b, :, cc * 128:(cc + 1) * 128].rearrange("(kc p) c -> p kc c", p=128), tw=w)
                    fw.dma("sp", g[:], self.gate[b][cc * 128:(cc + 1) * 128, t0:t0 + TT], tw=g)
                    p = self.nps()
                    for kc in range(6):
                        self.PE(lambda e: e.matmul(p[:, :], w[:, kc, :], brt[b][tt % 2][:, kc, :], start=(kc == 0), stop=(kc == 5)),
                                [w, brt[b][tt % 2]], [p], track=(kc == 5))
                    if b == 0:
                        self.V(lambda e: e.tensor_tensor(a_[:], p[:, :], g[:], ALU.mult), [p, g], [a_])
                    else:
                        t_ = tmp[wi % 2]
                        self.V(lambda e: e.tensor_tensor(t_[:], p[:, :], g[:], ALU.mult), [p, g], [t_])
                        if b < 3:
                            self.G(lambda e: e.tensor_tensor(a_[:], a_[:], t_[:], ALU.add), [a_, t_], [a_])
                        else:
                            self.G(lambda e: e.tensor_tensor(m_[:, cc, :], a_[:], t_[:], ALU.add), [a_, t_], [m_])
            xts = {}
            for nq in range(4):
                w = wo[nq % 2]
                fw.dma("sp", w[:], self.wb_out[l][:, nq * 512:(nq + 1) * 512].rearrange("(kc p) c -> p kc c", p=128), tw=w)
                for s in range(4):
                    p = self.nps()
                    for kc in range(16):
                        self.PE(lambda e: e.matmul(p[:, :], m_[:, kc, s * 128:(s + 1) * 128], w[:, kc, :], start=(kc == 0), stop=(kc == 15)),
                                [m_, w], [p], track=(kc == 15))
                    xt = xr[(nq * 4 + s) % 2]
                    r0 = t0 + s * 128
                    fw.dma("sp", xt[:, 0:512], self.xres[r0:r0 + 128, nq * 512:(nq + 1) * 512], tw=xt)
                    self.V(lambda e: e.tensor_tensor(xt[:, 0:512], xt[:, 0:512], p[:, :], ALU.add), [xt, p], [xt])
                    fw.dma("pool", self.xres[r0:r0 + 128, nq * 512:(nq + 1) * 512], xt[:, 0:512], tr=xt)
        fw.end()

    def ffn(self, l):
        fw, S = self.fw, self.S
        fw.begin()
        TT = 512
        self.rot = list(range(4))
        hTs = fw.ring("fhT", 2, [128, 16, TT], BF16)
        xr = fw.ring("fxr", 2, [128, D], F32)
        hb = fw.ring("fhb", 4, [128, D], BF16)
        sm = fw.ring("fsm", 2, [128, 2], F32)
        w1 = fw.ring("w1", 3, [128, 16, 256], BF16)
        w2 = fw.ring("w2", 2, [128, 11, 512], BF16)
        aT = fw.ring("aT", 1, [128, 44, TT], BF16)
        sg = fw.ring("sg", 2, [128, TT], F32)
        xo = fw.ring("xo", 2, [128, 512], F32)
        wi = 0
        for tt in range(S // TT):
            t0 = tt * TT
            hT = hTs[tt % 2]
            a_ = aT[0]
            self.build_hT(hT, t0, 4, R_FFN, xr, hb, sm)
            for fc2 in range(22):
                wg = w1[wi % 3]; wi += 1
                wu = w1[wi % 3]; wi += 1
                fw.dma("sp", wg[:], self.wb_f1[l][:, fc2 * 256:(fc2 + 1) * 256].rearrange("(kc p) c -> p kc c", p=128), tw=wg)
                fw.dma("sp", wu[:], self.wb_f1[l][:, FF + fc2 * 256:FF + (fc2 + 1) * 256].rearrange("(kc p) c -> p kc c", p=128), tw=wu)
                for j in range(2):
                    fc = fc2 * 2 + j
                    pg = self.nps()
                    for kc in range(16):
                        self.PE(lambda e: e.matmul(pg[:, :], wg[:, kc, j * 128:(j + 1) * 128], hT[:, kc, :], start=(kc == 0), stop=(kc == 15)),
                                [wg, hT], [pg], track=(kc == 15))
                    pu = self.nps()
                    for kc in range(16):
                        self.PE(lambda e: e.matmul(pu[:, :], wu[:, kc, j * 128:(j + 1) * 128], hT[:, kc, :], start=(kc == 0), stop=(kc == 15)),
                                [wu, hT], [pu], track=(kc == 15))
                    s_ = sg[fc % 2]
                    self.A(lambda e: e.activation(s_[:], pg[:, :], AF.Silu), [pg], [s_])
                    self.V(lambda e: e.tensor_tensor(a_[:, fc, :], s_[:], pu[:, :], ALU.mult), [s_, pu], [a_])
            for nq in range(4):
                ps4 = [self.ps[4 + j] for j in range(4)]
                for kq in range(4):
                    w = w2[(nq * 4 + kq) % 2]
                    fw.dma("sp", w[:], self.wb_f2[l][kq * 1408:(kq + 1) * 1408, nq * 512:(nq + 1) * 512].rearrange("(kc p) c -> p kc c", p=128), tw=w)
                    for s in range(4):
                        for kc in range(11):
                            last = (kq == 3 and kc == 10)
                            self.PE(lambda e: e.matmul(ps4[s][:, :], a_[:, kq * 11 + kc, s * 128:(s + 1) * 128], w[:, kc, :],
                                                       start=(kq == 0 and kc == 0), stop=last),
                                    [a_, w], [ps4[s]], track=(last or kc == 10))
                for s in range(4):
                    xt = xo[s % 2]
                    r0 = t0 + s * 128
                    fw.dma("sp", xt[:], self.xres[r0:r0 + 128, nq * 512:(nq + 1) * 512], tw=xt)
                    self.V(lambda e: e.tensor_tensor(xt[:], xt[:], ps4[s][:, :], ALU.add), [xt, ps4[s]], [xt])
                    fw.dma("pool", self.xres[r0:r0 + 128, nq * 512:(nq + 1) * 512], xt[:], tr=xt)
        fw.end()
        self.rot = list(range(8))

    def build(self, stop_after=None):
        self.consts()
        self.prepass()
        for l in range(self.L):
            self.load_params(l)
            self.phase_proj(l)
            if stop_after == "proj":
                break
            for gi, (win, dil) in enumerate(DILS):
                self.attention(self.qa[:, gi * 768:(gi + 1) * 768], self.ka[:, gi * 768:(gi + 1) * 768],
                               self.va[:, gi * 768:(gi + 1) * 768], 6, dil, gi * 6, "A", gi)
            self.merge_A()
            self.attention(self.qc, self.kc, self.vc, 2, 1, 18, "C", 0)
            self.lru(l)
            self.ssd(l)
            if stop_after == "mix":
                break
            self.merge_out(l)
            self.ffn(l)
        self.fw.barrier()
        return self.nc


def host_inputs(inp, L):
    f = lambda a: np.ascontiguousarray(np.asarray(a, dtype=np.float32))
    prow = np.zeros((L, NR), np.float32)
    prow[:, R_MIX:R_MIX + D] = inp["mix_norm"]
    prow[:, R_FFN:R_FFN + D] = inp["ffn_norm"]
    prow[:, R_QA:R_QA + 128] = inp["qk_norm_a"][:, 0]
    prow[:, R_KA:R_KA + 128] = inp["qk_norm_a"][:, 1]
    prow[:, R_QC:R_QC + 128] = inp["qk_norm_c"][:, 0]
    prow[:, R_KC:R_KC + 128] = inp["qk_norm_c"][:, 1]
    prow[:, R_SINK:R_SINK + 6] = inp["sinks_c"]
    prow[:, R_DTB:R_DTB + 12] = inp["ssm_dt_bias"]
    prow[:, R_ALOG:R_ALOG + 12] = inp["ssm_a_log"]
    prow[:, R_DSK:R_DSK + 12] = inp["ssm_d"]
    prow[:, R_SNORM:R_SNORM + 768] = inp["ssm_norm"]
    ppar = np.zeros((L, 128, NP_), np.float32)
    ppar[:, :, P_LCW:P_LCW + 24] = np.asarray(inp["conv_lru_w"]).reshape(L, 4, 6, 128).transpose(0, 3, 2, 1).reshape(L, 128, 24)
    ppar[:, :, P_LCB:P_LCB + 6] = np.asarray(inp["conv_lru_b"]).reshape(L, 6, 128).transpose(0, 2, 1)
    ppar[:, :, P_LGB:P_LGB + 12] = np.asarray(inp["lru_gate_b"]).reshape(L, 2, 6, 128).transpose(0, 3, 1, 2).reshape(L, 128, 12)
    ppar[:, :, P_LAM:P_LAM + 6] = np.asarray(inp["lru_lambda"]).reshape(L, 6, 128).transpose(0, 2, 1)
    ppar[:, :, P_SCW:P_SCW + 40] = np.asarray(inp["ssm_conv_w"]).reshape(L, 4, 10, 128).transpose(0, 3, 2, 1).reshape(L, 128, 40)
    ppar[:, :, P_SCB:P_SCB + 10] = np.asarray(inp["ssm_conv_b"]).reshape(L, 10, 128).transpose(0, 2, 1)
    common = {
        "w_in": f(inp["w_in"]), "w_branch": f(inp["w_branch"]), "w_out": f(inp["w_out"]),
        "w_ffn_in": f(inp["w_ffn_in"]), "w_ffn_out": f(inp["w_ffn_out"]),
        "lru_gate_w": f(inp["lru_gate_w"]), "prow": prow, "ppar": ppar,
        "btab": _bias_tables(np.asarray(inp["rel_bias_table"], dtype=np.float32)),
    }
    return common


LAYERS_PER_LAUNCH = 1


def kernel(**inputs):
    x = np.asarray(inputs["x"], dtype=np.float32)
    B, S, _ = x.shape
    L = inputs["w_in"].shape[0]
    common = host_inputs(inputs, L)
    LP = LAYERS_PER_LAUNCH
    nc = K(S, LP).build()
    per_layer = ("w_in", "w_branch", "w_out", "w_ffn_in", "w_ffn_out", "lru_gate_w", "prow", "ppar")
    cur = [np.ascontiguousarray(x[b]) for b in range(B)]
    for l0 in range(0, L, LP):
        cm = {k: (np.ascontiguousarray(v[l0:l0 + LP]) if k in per_layer else v) for k, v in common.items()}
        in_maps = [dict(cm, x=cur[b]) for b in range(B)]
        res = run_bass_kernel_spmd(nc, in_maps, core_ids=list(range(B)))
        cur = [np.ascontiguousarray(np.asarray(res.results[b]["out"], dtype=np.float32)) for b in range(B)]
    return np.stack(cur, axis=0)
```

```python
import contextlib
import numpy as np
import concourse.bass as bass
import concourse.mybir as mybir
from concourse.bass_utils import run_bass_kernel_spmd

F32 = mybir.dt.float32
BF16 = mybir.dt.bfloat16
AF = mybir.ActivationFunctionType
ALU = mybir.AluOpType
AX = mybir.AxisListType

SEG = 30000
DMA_SEG = 3500


class Ev:
    __slots__ = ("kind", "key", "idx", "sem", "val")

    def __init__(self, kind, key, idx, sem, val):
        self.kind, self.key, self.idx, self.sem, self.val = kind, key, idx, sem, val


class DSem:
    _n = 0

    def __init__(self, handle):
        self.h = handle
        self.cnt = 0
        DSem._n += 1
        self.id = DSem._n


class T:
    _n = 0

    def __init__(self, h, name):
        self.h = h
        self.name = name
        self.last_w = None
        self.readers = {}
        self.dsem = None
        T._n += 1
        self.id = T._n

    def __getitem__(self, idx):
        return self.h[idx]


class FW:
    def __init__(self, nc):
        self.nc = nc
        self.E = {"pe": nc.tensor, "act": nc.scalar, "dve": nc.vector,
                  "pool": nc.gpsimd, "sp": nc.sync}
        self.cnt = {k: 0 for k in self.E}
        self.esems = {k: [] for k in self.E}
        self.seen = {k: {} for k in self.E}
        self.pe_pending = []
        self.ptiles = []
        self.phtiles = []
        self.free_dsems = []
        self.stack = None
        self.uid = 0
        self.n_inst = 0

    def sbp(self, name, shape, dtype=F32):
        t = T(self.nc.alloc_sbuf_tensor(name, list(shape), dtype), name)
        self.ptiles.append(t)
        return t

    def psp(self, name, shape, dtype=F32):
        t = T(self.nc.alloc_psum_tensor(name, list(shape), dtype), name)
        self.ptiles.append(t)
        return t

    def begin(self):
        assert self.stack is None
        self.stack = contextlib.ExitStack()
        self.phtiles = []

    def sb(self, name, shape, dtype=F32):
        self.uid += 1
        nm = f"{name}_{self.uid}"
        h = self.stack.enter_context(self.nc.sbuf_tensor(nm, list(shape), dtype))
        t = T(h, nm)
        self.phtiles.append(t)
        return t

    def ring(self, name, n, shape, dtype=F32):
        return [self.sb(f"{name}{i}", shape, dtype) for i in range(n)]

    def end(self):
        self.barrier()
        for t in self.phtiles:
            if t.dsem is not None:
                self.free_dsems.append(t.dsem)
                t.dsem = None
        self.phtiles = []
        self.stack.close()
        self.stack = None

    def _latest(self, ek):
        idx = self.cnt[ek]
        s = (idx - 1) // SEG
        return Ev("eng", ek, idx, self.esems[ek][s], (idx - 1) % SEG + 1)

    def _new_eng_event(self, ek):
        self.cnt[ek] += 1
        s = (self.cnt[ek] - 1) // SEG
        while len(self.esems[ek]) <= s:
            self.esems[ek].append(self.nc.alloc_semaphore(f"e_{ek}_{len(self.esems[ek])}"))
        return self._latest(ek)

    def _wait(self, ek, evs):
        eng = self.E[ek]
        seen = self.seen[ek]
        best = {}
        for ev in evs:
            if ev is None:
                continue
            if ev.kind == "eng" and ev.key == ek and ek == "pe":
                continue
            if seen.get(ev.key, 0) >= ev.idx:
                continue
            b = best.get(ev.key)
            if b is None or b.idx < ev.idx:
                best[ev.key] = ev
        for key, ev in best.items():
            eng.wait_ge(ev.sem, ev.val)
            seen[key] = ev.idx

    def op(self, ek, build, reads=(), writes=(), track=True):
        evs = []
        rd = list(reads)
        if ek == "pe" and track and self.pe_pending:
            rd = rd + self.pe_pending
            self.pe_pending = []
        for t in rd:
            evs.append(t.last_w)
        for t in writes:
            evs.append(t.last_w)
            evs.extend(t.readers.values())
        self._wait(ek, evs)
        inst = build(self.E[ek])
        self.n_inst += 1
        if not track:
            assert ek == "pe"
            self.pe_pending.extend(rd)
            return inst
        ev = self._new_eng_event(ek)
        inst.then_inc(ev.sem, 1)
        for t in rd:
            t.readers[ek] = ev
        for t in writes:
            t.last_w = ev
            t.readers = {}
        return inst

    def _dsem(self, t):
        if t.dsem is not None and t.dsem.cnt >= DMA_SEG:
            t.dsem = None
        if t.dsem is None:
            while self.free_dsems:
                d = self.free_dsems.pop()
                if d.cnt < DMA_SEG:
                    t.dsem = d
                    break
            if t.dsem is None:
                t.dsem = DSem(self.nc.alloc_semaphore(f"d{DSem._n}"))
        return t.dsem

    def dma(self, qk, out, in_, tr=None, tw=None, **kw):
        t = tw if tw is not None else tr
        evs = []
        if tw is not None:
            evs.append(tw.last_w)
            evs.extend(tw.readers.values())
        if tr is not None:
            evs.append(tr.last_w)
        self._wait(qk, evs)
        inst = self.E[qk].dma_start(out=out, in_=in_, **kw)
        self.n_inst += 1
        d = self._dsem(t)
        d.cnt += 1
        inst.then_inc(d.h, 16)
        ev = Ev("dma", ("d", d.id), d.cnt, d.h, 16 * d.cnt)
        if tw is not None:
            tw.last_w = ev
            tw.readers = {}
        if tr is not None:
            tr.readers[ev.key] = ev
        return inst

    def barrier(self):
        assert not self.pe_pending
        evs = []
        for ek in self.E:
            if self.cnt[ek] > 0:
                evs.append(self._latest(ek))
        for t in self.ptiles + self.phtiles:
            if t.last_w is not None and t.last_w.kind == "dma":
                evs.append(t.last_w)
            for ev in t.readers.values():
                if ev.kind == "dma":
                    evs.append(ev)
        for ek in self.E:
            self._wait(ek, [e for e in evs if not (e.kind == "eng" and e.key == ek)])


D = 2048
NIN = 19980
FF = 5632
C_QA, C_KA, C_VA = 0, 2304, 4608
C_QC, C_KC, C_VC = 6912, 7680, 7936
C_LX, C_LG, C_Z, C_XBC, C_DT, C_G = 8192, 8960, 9728, 10496, 11776, 11788
DILS = ((128, 1), (512, 4), (2048, 16))
R_MIX, R_FFN, R_QA, R_KA, R_QC, R_KC, R_SINK, R_DTB, R_ALOG, R_DSK, R_SNORM, NR = \
    0, 2048, 4096, 4224, 4352, 4480, 4608, 4614, 4626, 4638, 4650, 5418
P_LCW, P_LCB, P_LGB, P_LAM, P_SCW, P_SCB, NP_ = 0, 24, 30, 42, 48, 88, 98


def _t5_bucket(dist):
    max_exact = 16
    safe = np.maximum(dist, 1).astype(np.float32)
    large = max_exact + (np.log(safe / max_exact) / np.log(2048 / max_exact) * 16).astype(np.int32)
    return np.where(dist < max_exact, dist, np.minimum(large, 31)).astype(np.int32)


def _bias_tables(tab):
    qi = np.arange(128)[:, None]
    kj = np.arange(256)[None, :]
    dist = 128 + qi - kj
    out = np.empty((128, 24, 256), np.float32)
    for gi, (win, dil) in enumerate(DILS):
        valid = (dist >= 0) & (dist <= win // dil)
        bk = _t5_bucket(np.clip(dist, 0, None) * dil)
        for h in range(6):
            out[:, gi * 6 + h, :] = np.where(valid, tab[bk, gi * 6 + h], np.float32(-1e30))
    valid = (dist >= 0) & (dist <= 127)
    bk = _t5_bucket(np.clip(dist, 0, None))
    for h in range(6):
        out[:, 18 + h, :] = np.where(valid, tab[bk, 18 + h], np.float32(-1e30))
    return out


class K:
    def __init__(self, S, L, taps=()):
        self.S, self.L, self.taps = S, L, set(taps)
        nc = self.nc = bass.Bass("TRN2", target_bir_lowering=False)
        self.fw = FW(nc)

        def din(name, shape, dt=F32):
            return nc.dram_tensor(name, list(shape), dt, kind="ExternalInput").ap()

        def dsc(name, shape, dt=F32):
            kind = "ExternalOutput" if name in self.taps else "Internal"
            return nc.dram_tensor(name, list(shape), dt, kind=kind).ap()

        self.x_in = din("x", [S, D])
        self.w_in = din("w_in", [L, D, NIN])
        self.w_br = din("w_branch", [L, 4, 768, D])
        self.w_out = din("w_out", [L, D, D])
        self.w_f1 = din("w_ffn_in", [L, D, 2 * FF])
        self.w_f2 = din("w_ffn_out", [L, FF, D])
        self.lgw = din("lru_gate_w", [L, 2, 6, 128, 128])
        self.prow = din("prow", [L, NR])
        self.ppar = din("ppar", [L, 128, NP_])
        self.btab = din("btab", [128, 24, 256])
        self.out = nc.dram_tensor("out", [S, D], F32, kind="ExternalOutput").ap()
        self.wb_in = [dsc(f"wb_in{l}", [D, NIN], BF16) for l in range(L)]
        self.wb_br = [dsc(f"wb_br{l}", [4, 768, D], BF16) for l in range(L)]
        self.wb_out = [dsc(f"wb_out{l}", [D, D], BF16) for l in range(L)]
        self.wb_f1 = [dsc(f"wb_f1{l}", [D, 2 * FF], BF16) for l in range(L)]
        self.wb_f2 = [dsc(f"wb_f2{l}", [FF, D], BF16) for l in range(L)]
        self.xres = self.out
        self.qa = dsc("qa", [S, 2304], BF16)
        self.ka = dsc("ka", [S, 2304], BF16)
        self.va = dsc("va", [S, 2304], BF16)
        self.qc = dsc("qc", [S, 768], BF16)
        self.kc = dsc("kc", [S, 256], BF16)
        self.vc = dsc("vc", [S, 256], BF16)
        self.lrux = dsc("lrux", [768, 4 + S])
        self.lrug = dsc("lrug", [768, S])
        self.z = dsc("z", [S, 768])
        self.xbc = dsc("xbc", [1280, 4 + S])
        self.dt = dsc("dt", [S, 12])
        self.gate = [dsc(f"gate{b}", [2048, S], BF16) for b in range(4)]
        self.oA = dsc("oA", [3, S, 780])
        self.brT = dsc("brT", [4, 768, S], BF16)

    def V(self, fn, r, w):
        return self.fw.op("dve", fn, reads=r, writes=w)

    def A(self, fn, r, w):
        return self.fw.op("act", fn, reads=r, writes=w)

    def G(self, fn, r, w):
        return self.fw.op("pool", fn, reads=r, writes=w)

    def PE(self, fn, r, w, track=True):
        return self.fw.op("pe", fn, reads=r, writes=w, track=track)

    def psb(self, i, n):
        return self.ps[i][:, :].bitcast(BF16)[:, 0:n]

    def consts(self):
        fw = self.fw
        self.ps = [fw.psp(f"psb{i}", [128, 512], F32) for i in range(8)]
        self.psn = 0
        self.rot = list(range(8))
        idf = self.idf = fw.sbp("idf", [128, 128], F32)
        self.G(lambda e: e.memset(idf[:], 0.0), [], [idf])
        self.G(lambda e: e.affine_select(idf[:], idf[:], pattern=[[-1, 128]], compare_op=ALU.not_equal,
                                         fill=1.0, base=0, channel_multiplier=1), [idf], [idf])
        idb = self.idb = fw.sbp("idb", [128, 128], BF16)
        self.V(lambda e: e.tensor_copy(idb[:], idf[:]), [idf], [idb])
        tri = self.tri = fw.sbp("tri", [128, 128], F32)
        self.G(lambda e: e.memset(tri[:], 1.0), [], [tri])
        self.G(lambda e: e.affine_select(tri[:], tri[:], pattern=[[1, 128]], compare_op=ALU.is_ge,
                                         fill=0.0, base=0, channel_multiplier=-1), [tri], [tri])
        mgt = self.mgt = fw.sbp("mgt", [128, 128], F32)
        self.G(lambda e: e.memset(mgt[:], 1.0), [], [mgt])
        self.G(lambda e: e.affine_select(mgt[:], mgt[:], pattern=[[-1, 128]], compare_op=ALU.is_ge,
                                         fill=0.0, base=-1, channel_multiplier=1), [mgt], [mgt])
        ones = self.ones = fw.sbp("ones", [128, 128], F32)
        self.G(lambda e: e.memset(ones[:], 1.0), [], [ones])
        self.dmy = fw.sbp("dmy", [128, 4], F32)
        self.zer = fw.sbp("zer", [128, 8], F32)
        self.G(lambda e: e.memset(self.zer[:], 0.0), [], [self.zer])
        self.rowp = fw.sbp("rowp", [128, NR], F32)
        self.parp = fw.sbp("parp", [128, NP_], F32)
        self.gqs = fw.sbp("gqs", [128, 256], F32)
        self.lruc = fw.sbp("lruc", [128, 12], F32)
        self.aneg = fw.sbp("aneg", [128, 12], F32)

    def nps(self):
        self.psn = (self.psn + 1) % len(self.rot)
        return self.ps[self.rot[self.psn]]

    def load_params(self, l):
        fw = self.fw
        rowp, parp = self.rowp, self.parp
        fw.dma("sp", rowp[:], self.prow[l:l + 1, :].partition_broadcast(128), tw=rowp)
        fw.dma("sp", parp[:], self.ppar[l], tw=parp)
        gqs = self.gqs
        sc = float(128 ** -0.5)
        self.V(lambda e: e.tensor_scalar(gqs[:, 0:128], rowp[:, R_QA:R_QA + 128], sc, None, ALU.mult), [rowp], [gqs])
        self.V(lambda e: e.tensor_scalar(gqs[:, 128:256], rowp[:, R_QC:R_QC + 128], sc, None, ALU.mult), [rowp], [gqs])
        lruc = self.lruc
        fw.begin()
        y = fw.sb("spy", [128, 6]); ab = fw.sb("spa", [128, 6]); ex = fw.sb("spe", [128, 6])
        lam = parp[:, P_LAM:P_LAM + 6]
        self.V(lambda e: e.tensor_scalar(y[:], lam, -1.0, None, ALU.mult), [parp], [y])
        self.A(lambda e: e.activation(ab[:], y[:], AF.Abs), [y], [ab])
        self.A(lambda e: e.activation(ex[:], ab[:], AF.Exp, scale=-1.0), [ab], [ex])
        self.A(lambda e: e.activation(ex[:], ex[:], AF.Ln, bias=1.0, scale=1.0), [ex], [ex])
        self.V(lambda e: e.tensor_scalar(y[:], y[:], 0.0, None, ALU.max), [y], [y])
        self.V(lambda e: e.tensor_tensor(y[:], y[:], ex[:], ALU.add), [y, ex], [y])
        self.V(lambda e: e.tensor_scalar(lruc[:, 0:6], y[:], -8.0, None, ALU.mult), [y], [lruc])
        self.V(lambda e: e.tensor_scalar(lruc[:, 6:12], y[:], -16.0, None, ALU.mult), [y], [lruc])
        aneg = self.aneg
        self.A(lambda e: e.activation(aneg[:], rowp[:, R_ALOG:R_ALOG + 12], AF.Exp), [rowp], [aneg])
        self.V(lambda e: e.tensor_scalar(aneg[:], aneg[:], -1.0, None, ALU.mult), [aneg], [aneg])
        fw.end()

    def prepass(self):
        fw, S, L = self.fw, self.S, self.L
        dmy = self.dmy
        for l in range(L):
            for r0 in range(0, D, 256):
                fw.dma("pool", self.wb_in[l][r0:r0 + 256, :], self.w_in[l, r0:r0 + 256, :], tr=dmy)
                fw.dma("pool", self.wb_f1[l][r0:r0 + 256, :], self.w_f1[l, r0:r0 + 256, :], tr=dmy)
            for r0 in range(0, D, 1024):
                fw.dma("pool", self.wb_out[l][r0:r0 + 1024, :], self.w_out[l, r0:r0 + 1024, :], tr=dmy)
            for b in range(4):
                fw.dma("pool", self.wb_br[l][b], self.w_br[l, b], tr=dmy)
            for r0 in range(0, FF, 1408):
                fw.dma("pool", self.wb_f2[l][r0:r0 + 1408, :], self.w_f2[l, r0:r0 + 1408, :], tr=dmy)
        for r0 in range(0, S, 1024):
            fw.dma("sp", self.xres[r0:r0 + 1024, :], self.x_in[r0:r0 + 1024, :], tr=dmy)
        zer = self.zer
        for cb in range(6):
            fw.dma("sp", self.lrux[cb * 128:(cb + 1) * 128, 0:4], zer[:, 0:4], tr=zer)
        for cb in range(10):
            fw.dma("sp", self.xbc[cb * 128:(cb + 1) * 128, 0:4], zer[:, 0:4], tr=zer)
        fw.barrier()

    def build_hT(self, hT, t0, n_sub, grow, xr, hb, sm):
        fw = self.fw
        rowp = self.rowp
        for s in range(n_sub):
            xt = xr[s % len(xr)]
            fw.dma("sp", xt[:], self.xres[t0 + s * 128:t0 + (s + 1) * 128, :], tw=xt)
            h = hb[s]
            ss = sm[s % len(sm)]
            self.A(lambda e: e.activation(h[:], xt[:], AF.Square, accum_out=ss[:, 0:1]), [xt], [h, ss])
            self.A(lambda e: e.activation(ss[:, 1:2], ss[:, 0:1], AF.Sqrt, bias=1e-6, scale=1.0 / D), [ss], [ss])
            self.V(lambda e: e.reciprocal(ss[:, 1:2], ss[:, 1:2]), [ss], [ss])
            self.V(lambda e: e.scalar_tensor_tensor(h[:], xt[:], ss[:, 1:2], rowp[:, grow:grow + D],
                                                    ALU.mult, ALU.mult), [xt, ss, rowp], [h])
        for kc in range(16):
            for g0 in range(0, n_sub, 4):
                ng = min(4, n_sub - g0)
                p = self.nps()
                pv = p[:, :].bitcast(BF16)
                for s in range(g0, g0 + ng):
                    self.PE(lambda e: e.transpose(pv[:, (s - g0) * 128:(s - g0 + 1) * 128],
                                                  hb[s][:, kc * 128:(kc + 1) * 128], self.idb[:]),
                            [hb[s], self.idb], [p])
                dst = hT[:, kc, g0 * 128:(g0 + ng) * 128]
                if kc % 2 == 0:
                    self.A(lambda e: e.copy(dst, pv[:, 0:ng * 128]), [p], [hT])
                else:
                    self.V(lambda e: e.tensor_copy(dst, pv[:, 0:ng * 128]), [p], [hT])

    def phase_proj(self, l):
        fw, S = self.fw, self.S
        rowp = self.rowp
        fw.begin()
        TT = 512
        hTs = fw.ring("hT", 2, [128, 16, TT], BF16)
        wr = fw.ring("wr", 3, [128, 16, 512], BF16)
        xr = fw.ring("xr", 2, [128, D], F32)
        hb = fw.ring("hb", 4, [128, D], BF16)
        sm = fw.ring("sm", 2, [128, 2], F32)
        sq = fw.sb("sq", [128, 512], F32)
        tmp = fw.ring("tmp", 2, [128, 512], F32)
        obf = fw.ring("obf", 3, [128, 512], BF16)
        of32 = fw.ring("of32", 3, [128, 512], F32)
        ssr = fw.ring("ssr", 2, [128, 8], F32)
        dts = fw.ring("dts", 2, [128, 48], F32)
        tiles = []
        for i in range(6):
            tiles.append((C_QA + i * 384, 384, "qk", (self.qa, i * 384, self.gqs, 0)))
        for i in range(6):
            tiles.append((C_KA + i * 384, 384, "qk", (self.ka, i * 384, rowp, R_KA)))
        for i in range(6):
            tiles.append((C_VA + i * 384, 384, "cbf", (self.va, i * 384)))
        for i in range(2):
            tiles.append((C_QC + i * 384, 384, "qk", (self.qc, i * 384, self.gqs, 128)))
        tiles.append((C_KC, 256, "qk", (self.kc, 0, rowp, R_KC)))
        tiles.append((C_VC, 256, "cbf", (self.vc, 0)))
        for i in range(2):
            tiles.append((C_LX + i * 384, 384, "fm", ("lrux", i * 384)))
        for i in range(2):
            tiles.append((C_LG + i * 384, 384, "fm", ("lrug", i * 384)))
        for i in range(2):
            tiles.append((C_Z + i * 384, 384, "cf32", (self.z, i * 384)))
        tiles.append((C_XBC, 512, "fm", ("xbc", 0)))
        tiles.append((C_XBC + 512, 512, "fm", ("xbc", 512)))
        tiles.append((C_XBC + 1024, 256, "fm", ("xbc", 1024)))
        tiles.append((C_DT, 12, "dt", None))
        for i in range(16):
            tiles.append((C_G + i * 512, 512, "fm", ("gate", i * 512)))
        wi = 0
        oi = 0
        for tt in range(S // TT):
            t0 = tt * TT
            hT = hTs[tt % 2]
            self.build_hT(hT, t0, 4, R_MIX, xr, hb, sm)
            for (c0, ncw, kind, arg) in tiles:
                wt = wr[wi % 3]
                wi += 1
                fw.dma("sp", wt[:, :, 0:ncw],
                       self.wb_in[l][:, c0:c0 + ncw].rearrange("(kc p) c -> p kc c", p=128), tw=wt)
                if kind == "fm":
                    name, d0 = arg
                    for cc in range(ncw // 128):
                        p = self.nps()
                        for kc in range(16):
                            self.PE(lambda e: e.matmul(p[:, :], wt[:, kc, cc * 128:(cc + 1) * 128], hT[:, kc, :],
                                                       start=(kc == 0), stop=(kc == 15)),
                                    [wt, hT], [p], track=(kc == 15))
                        oi += 1
                        ch0 = d0 + cc * 128
                        if name == "gate":
                            o = obf[oi % 3]
                            self.A(lambda e: e.activation(o[:], p[:, :], AF.Sigmoid), [p], [o])
                            fw.dma("pool", self.gate[ch0 // 2048][ch0 % 2048:ch0 % 2048 + 128, t0:t0 + TT], o[:], tr=o)
                        elif name == "lrug":
                            o = of32[oi % 3]
                            self.A(lambda e: e.activation(o[:], p[:, :], AF.Gelu_apprx_tanh), [p], [o])
                            fw.dma("pool", self.lrug[ch0:ch0 + 128, t0:t0 + TT], o[:], tr=o)
                        else:
                            o = of32[oi % 3]
                            self.V(lambda e: e.tensor_copy(o[:], p[:, :]), [p], [o])
                            dst = self.lrux if name == "lrux" else self.xbc
                            fw.dma("pool", dst[ch0:ch0 + 128, 4 + t0:4 + t0 + TT], o[:], tr=o)
                    continue
                for s in range(4):
                    p = self.nps()
                    for kc in range(16):
                        self.PE(lambda e: e.matmul(p[:, 0:ncw], hT[:, kc, s * 128:(s + 1) * 128], wt[:, kc, 0:ncw],
                                                   start=(kc == 0), stop=(kc == 15)),
                                [wt, hT], [p], track=(kc == 15))
                    oi += 1
                    r0 = t0 + s * 128
                    if kind == "qk":
                        dst, d0, gt, goff = arg
                        nh = ncw // 128
                        ss = ssr[oi % 2]
                        tm = tmp[oi % 2]
                        o = obf[oi % 3]
                        self.A(lambda e: e.activation(sq[:, 0:ncw], p[:, 0:ncw], AF.Square), [p], [sq])
                        self.V(lambda e: e.tensor_reduce(ss[:, 0:nh], sq[:, 0:ncw].rearrange("p (h d) -> p h d", h=nh),
                                                         AX.X, ALU.add), [sq], [ss])
                        self.A(lambda e: e.activation(ss[:, 4:4 + nh], ss[:, 0:nh], AF.Sqrt, bias=1e-6, scale=1.0 / 128),
                               [ss], [ss])
                        self.V(lambda e: e.reciprocal(ss[:, 4:4 + nh], ss[:, 4:4 + nh]), [ss], [ss])
                        self.V(lambda e: e.tensor_tensor(tm[:, 0:ncw].rearrange("p (h d) -> p h d", h=nh),
                                                         p[:, 0:ncw].rearrange("p (h d) -> p h d", h=nh),
                                                         ss[:, 4:4 + nh].unsqueeze(2).to_broadcast([128, nh, 128]),
                                                         ALU.mult), [p, ss], [tm])
                        self.G(lambda e: e.tensor_tensor(o[:, 0:ncw].rearrange("p (h d) -> p h d", h=nh),
                                                         tm[:, 0:ncw].rearrange("p (h d) -> p h d", h=nh),
                                                         gt[:, goff:goff + 128].unsqueeze(1).to_broadcast([128, nh, 128]),
                                                         ALU.mult), [tm, gt], [o])
                        fw.dma("pool", dst[r0:r0 + 128, d0:d0 + ncw], o[:, 0:ncw], tr=o)
                    elif kind == "cbf":
                        dst, d0 = arg
                        o = obf[oi % 3]
                        self.A(lambda e: e.copy(o[:, 0:ncw], p[:, 0:ncw]), [p], [o])
                        fw.dma("pool", dst[r0:r0 + 128, d0:d0 + ncw], o[:, 0:ncw], tr=o)
                    elif kind == "cf32":
                        dst, d0 = arg
                        o = of32[oi % 3]
                        self.V(lambda e: e.tensor_copy(o[:, 0:ncw], p[:, 0:ncw]), [p], [o])
                        fw.dma("pool", dst[r0:r0 + 128, d0:d0 + ncw], o[:, 0:ncw], tr=o)
                    elif kind == "dt":
                        d = dts[oi % 2]
                        x_, ab, ex, ou = d[:, 0:12], d[:, 12:24], d[:, 24:36], d[:, 36:48]
                        self.V(lambda e: e.tensor_tensor(x_, p[:, 0:12], rowp[:, R_DTB:R_DTB + 12], ALU.add), [p, rowp], [d])
                        self.A(lambda e: e.activation(ab, x_, AF.Abs), [d], [d])
                        self.A(lambda e: e.activation(ex, ab, AF.Exp, scale=-1.0), [d], [d])
                        self.A(lambda e: e.activation(ex, ex, AF.Ln, bias=1.0, scale=1.0), [d], [d])
                        self.V(lambda e: e.scalar_tensor_tensor(ou, x_, 0.0, ex, ALU.max, ALU.add), [d], [d])
                        fw.dma("pool", self.dt[r0:r0 + 128, :], ou, tr=d)
        fw.end()

    def store_fm(self, otok, br, tok0, stg):
        fw = self.fw
        for h in range(6):
            p = self.nps()
            pv = p[:, :].bitcast(BF16)
            for j in range(4):
                self.PE(lambda e: e.transpose(pv[:, j * 128:(j + 1) * 128], otok[j][:, h * 128:(h + 1) * 128], self.idb[:]),
                        [otok[j], self.idb], [p])
            if h % 2 == 0:
                self.A(lambda e: e.copy(stg[:, h, :], pv[:, 0:512]), [p], [stg])
            else:
                self.V(lambda e: e.tensor_copy(stg[:, h, :], pv[:, 0:512]), [p], [stg])
        fw.dma("pool", self.brT[br, :, tok0:tok0 + 512].rearrange("(h p) t -> p h t", p=128), stg[:], tr=stg)

    def attention(self, qd, kd, vd, nkv, dil, tabbase, mode, gi):
        fw, S = self.fw, self.S
        rowp = self.rowp
        fw.begin()
        self.rot = [7]
        SB = [self.ps[0], self.ps[1], self.ps[2]]
        PT = [self.ps[3], self.ps[4]]
        OB = [self.ps[5], self.ps[6]]
        btab = fw.sb("btab", [128, 6, 256], F32)
        fw.dma("sp", btab[:], self.btab[:, tabbase:tabbase + 6, :], tw=btab)
        qr = fw.ring("qr", 2, [128, 768], BF16)
        kr = fw.ring("kr", 2, [128, nkv * 128], BF16)
        vr = fw.ring("vr", 3, [128, nkv * 128], BF16)
        kT = fw.ring("kT", 3, [128, nkv, 128], BF16)
        qT = fw.ring("qT", 2, [128, 6, 128], BF16)
        ssb = fw.ring("ssb", 2, [128, 6, 256], F32)
        pb = fw.ring("pb", 2, [128, 6, 256], BF16)
        pT = fw.ring("pT", 2, [128, 12, 128], BF16)
        ng = fw.ring("ng", 2, [128, 16], F32)
        mls = fw.ring("mls", 2, [128, 12], F32)
        if mode == "A":
            ost = fw.ring("ost", 2, [128, 768], F32)
        else:
            otok = fw.ring("otok", 8, [128, 768], BF16)
            stg = fw.ring("stg", 2, [128, 6, 512], BF16)
        nblk = S // dil // 128
        it = 0
        for r in range(dil):
            for b in range(nblk):
                row0 = r + dil * 128 * b
                rows = slice(row0, row0 + dil * 127 + 1, dil)
                q = qr[it % 2]; k = kr[it % 2]; v = vr[it % 3]; kTc = kT[it % 3]
                vp = vr[(it - 1) % 3]; kTp = kT[(it - 1) % 3]
                qt = qT[it % 2]; sb_ = ssb[it % 2]; pp = pb[it % 2]; pt = pT[it % 2]
                n_ = ng[it % 2]; ms = mls[it % 2]
                fw.dma("sp", q[:], qd[rows, :], tw=q)
                fw.dma("sp", k[:], kd[rows, :], tw=k)
                fw.dma("sp", v[:], vd[rows, :], tw=v)
                lo = 128 if b == 0 else 0
                p = self.nps()
                pv = p[:, :].bitcast(BF16)
                for hk in range(nkv):
                    self.PE(lambda e: e.transpose(pv[:, hk * 128:(hk + 1) * 128], k[:, hk * 128:(hk + 1) * 128], self.idb[:]), [k, self.idb], [p])
                self.A(lambda e: e.copy(kTc[:, :, :].rearrange("p a b -> p (a b)"), pv[:, 0:nkv * 128]), [p], [kTc])
                for h in range(6):
                    self.PE(lambda e: e.transpose(pv[:, h * 128:(h + 1) * 128], q[:, h * 128:(h + 1) * 128], self.idb[:]), [q, self.idb], [p])
                self.V(lambda e: e.tensor_copy(qt[:, :, :].rearrange("p a b -> p (a b)"), pv[:, 0:768]), [p], [qt])
                for h in range(6):
                    hk = h if nkv == 6 else h // 3
                    sp_ = SB[h // 2]
                    o0 = (h % 2) * 256
                    if b > 0:
                        self.PE(lambda e: e.matmul(sp_[:, o0:o0 + 128], qt[:, h, :], kTp[:, hk, :], start=True, stop=True), [qt, kTp], [sp_])
                    self.PE(lambda e: e.matmul(sp_[:, o0 + 128:o0 + 256], qt[:, h, :], kTc[:, hk, :], start=True, stop=True), [qt, kTc], [sp_])
                for j in range(3):
                    self.V(lambda e: e.tensor_tensor(sb_[:, 2 * j:2 * j + 2, lo:256],
                                                     SB[j][:, :].rearrange("p (a b) -> p a b", a=2)[:, :, lo:256],
                                                     btab[:, 2 * j:2 * j + 2, lo:256], ALU.add), [SB[j], btab], [sb_])
                self.V(lambda e: e.reduce_max(ms[:, 0:6], sb_[:, :, lo:256], AX.X), [sb_], [ms])
                if mode == "C":
                    self.V(lambda e: e.tensor_tensor(ms[:, 0:6], ms[:, 0:6], rowp[:, R_SINK:R_SINK + 6], ALU.max), [ms, rowp], [ms])
                self.V(lambda e: e.tensor_scalar(n_[:, 0:6], ms[:, 0:6], -1.0, None, ALU.mult), [ms], [n_])
                for h in range(6):
                    self.A(lambda e: e.activation(pp[:, h, lo:256], sb_[:, h, lo:256], AF.Exp, bias=n_[:, h:h + 1], scale=1.0,
                                                  accum_out=ms[:, 6 + h:7 + h]), [sb_, n_], [pp, ms])
                if mode == "C":
                    self.V(lambda e: e.tensor_tensor(n_[:, 8:14], rowp[:, R_SINK:R_SINK + 6], ms[:, 0:6], ALU.subtract), [rowp, ms], [n_])
                    self.A(lambda e: e.activation(n_[:, 8:14], n_[:, 8:14], AF.Exp), [n_], [n_])
                    self.V(lambda e: e.tensor_tensor(ms[:, 6:12], ms[:, 6:12], n_[:, 8:14], ALU.add), [ms, n_], [ms])
                    self.V(lambda e: e.reciprocal(ms[:, 6:12], ms[:, 6:12]), [ms], [ms])
                tv = [PT[0][:, :].bitcast(BF16), PT[1][:, :].bitcast(BF16)]
                for h in range(6):
                    tb, slot = (0, h) if h < 4 else (1, h - 4)
                    for half in range(2):
                        if half == 0 and b == 0:
                            continue
                        c0_ = (slot * 2 + half) * 128
                        self.PE(lambda e: e.transpose(tv[tb][:, c0_:c0_ + 128], pp[:, h, half * 128:(half + 1) * 128], self.idb[:]), [pp, self.idb], [PT[tb]])
                ptf = pt[:, :, :].rearrange("p a b -> p (a b)")
                self.A(lambda e: e.copy(ptf[:, 0:1024], tv[0][:, 0:1024]), [PT[0]], [pt])
                self.V(lambda e: e.tensor_copy(ptf[:, 1024:1536], tv[1][:, 0:512]), [PT[1]], [pt])
                for h in range(6):
                    hk = h if nkv == 6 else h // 3
                    ob, oc = (OB[0], h * 128) if h < 4 else (OB[1], (h - 4) * 128)
                    if b > 0:
                        self.PE(lambda e: e.matmul(ob[:, oc:oc + 128], pt[:, 2 * h, :], vp[:, hk * 128:(hk + 1) * 128], start=True, stop=False), [pt, vp], [ob], track=False)
                    self.PE(lambda e: e.matmul(ob[:, oc:oc + 128], pt[:, 2 * h + 1, :], v[:, hk * 128:(hk + 1) * 128], start=(b == 0), stop=True), [pt, v], [ob])
                if mode == "A":
                    os_ = ost[it % 2]
                    self.V(lambda e: e.tensor_copy(os_[:, 0:512], OB[0][:, 0:512]), [OB[0]], [os_])
                    self.A(lambda e: e.copy(os_[:, 512:768], OB[1][:, 0:256]), [OB[1]], [os_])
                    fw.dma("pool", self.oA[gi, rows, 0:768], os_[:], tr=os_)
                    fw.dma("pool", self.oA[gi, rows, 768:780], ms[:], tr=ms)
                else:
                    ot = otok[it % 8]
                    self.V(lambda e: e.tensor_tensor(ot[:, 0:512].rearrange("p (h d) -> p h d", h=4), OB[0][:, 0:512].rearrange("p (h d) -> p h d", h=4),
                                                     ms[:, 6:10].unsqueeze(2).to_broadcast([128, 4, 128]), ALU.mult), [OB[0], ms], [ot])
                    self.V(lambda e: e.tensor_tensor(ot[:, 512:768].rearrange("p (h d) -> p h d", h=2), OB[1][:, 0:256].rearrange("p (h d) -> p h d", h=2),
                                                     ms[:, 10:12].unsqueeze(2).to_broadcast([128, 2, 128]), ALU.mult), [OB[1], ms], [ot])
                    if b % 4 == 3:
                        self.store_fm([otok[(it - 3 + j) % 8] for j in range(4)], 2, (b - 3) * 128, stg[(b // 4) % 2])
                it += 1
        fw.end()
        self.rot = list(range(8))

    def merge_A(self):
        fw, S = self.fw, self.S
        fw.begin()
        og = [fw.ring(f"og{g}", 2, [128, 780], F32) for g in range(3)]
        sm = fw.ring("msm", 2, [128, 64], F32)
        acc = fw.ring("macc", 2, [128, 768], F32)
        tm = fw.ring("mtm", 2, [128, 768], F32)
        otok = fw.ring("motok", 8, [128, 768], BF16)
        stg = fw.ring("mstg", 2, [128, 6, 512], BF16)
        for i in range(S // 128):
            o = [og[g][i % 2] for g in range(3)]
            for g in range(3):
                fw.dma("sp", o[g][:], self.oA[g, i * 128:(i + 1) * 128, :], tw=o[g])
            s_ = sm[i % 2]
            M, e_, w_, dn = s_[:, 0:6], s_[:, 8:26], s_[:, 26:44], s_[:, 44:50]
            self.V(lambda e: e.tensor_tensor(M, o[0][:, 768:774], o[1][:, 768:774], ALU.max), [o[0], o[1]], [s_])
            self.V(lambda e: e.tensor_tensor(M, M, o[2][:, 768:774], ALU.max), [s_, o[2]], [s_])
            for g in range(3):
                self.V(lambda e: e.tensor_tensor(e_[:, g * 6:(g + 1) * 6], o[g][:, 768:774], M, ALU.subtract), [o[g], s_], [s_])
            self.A(lambda e: e.activation(e_, e_, AF.Exp), [s_], [s_])
            for g in range(3):
                self.V(lambda e: e.tensor_tensor(w_[:, g * 6:(g + 1) * 6], e_[:, g * 6:(g + 1) * 6], o[g][:, 774:780], ALU.mult), [o[g], s_], [s_])
            self.V(lambda e: e.tensor_tensor(dn, w_[:, 0:6], w_[:, 6:12], ALU.add), [s_], [s_])
            self.V(lambda e: e.tensor_tensor(dn, dn, w_[:, 12:18], ALU.add), [s_], [s_])
            self.V(lambda e: e.reciprocal(dn, dn), [s_], [s_])
            for g in range(3):
                self.V(lambda e: e.tensor_tensor(e_[:, g * 6:(g + 1) * 6], e_[:, g * 6:(g + 1) * 6], dn, ALU.mult), [s_], [s_])
            a_ = acc[i % 2]; t_ = tm[i % 2]; ot = otok[i % 8]

            def bc(g):
                return e_[:, g * 6:(g + 1) * 6].unsqueeze(2).to_broadcast([128, 6, 128])

            def v3(t, n=768):
                return t[:, 0:768].rearrange("p (h d) -> p h d", h=6)
            self.V(lambda e: e.tensor_tensor(v3(a_), v3(o[0]), bc(0), ALU.mult), [o[0], s_], [a_])
            self.G(lambda e: e.tensor_tensor(v3(t_), v3(o[1]), bc(1), ALU.mult), [o[1], s_], [t_])
            self.V(lambda e: e.tensor_tensor(a_[:], a_[:], t_[:], ALU.add), [a_, t_], [a_])
            self.G(lambda e: e.tensor_tensor(v3(t_), v3(o[2]), bc(2), ALU.mult), [o[2], s_], [t_])
            self.V(lambda e: e.tensor_tensor(ot[:], a_[:], t_[:], ALU.add), [a_, t_], [ot])
            if i % 4 == 3:
                self.store_fm([otok[(i - 3 + j) % 8] for j in range(4)], 0, (i - 3) * 128, stg[(i // 4) % 2])
        fw.end()

    def lru(self, l):
        fw, S = self.fw, self.S
        parp = self.parp
        fw.begin()
        TL = 1024
        gw = fw.sb("gw", [128, 12, 128], F32)
        fw.dma("sp", gw[:], self.lgw[l].rearrange("g h i o -> i (g h) o"), tw=gw)
        xin = fw.ring("lx", 2, [128, 3 + TL], F32)
        xc = fw.ring("lxc", 2, [128, TL], F32)
        rg = fw.ring("lr", 2, [128, TL], F32)
        ig = fw.ring("li", 2, [128, TL], F32)
        aa = fw.ring("la", 2, [128, TL], F32)
        uu = fw.ring("lu", 2, [128, TL], F32)
        hh = fw.ring("lh", 2, [128, TL], F32)
        gg = fw.ring("lg", 2, [128, TL], F32)
        ob = fw.ring("lob", 2, [128, TL], BF16)
        it = 0
        for cb in range(6):
            for tt in range(S // TL):
                t0 = tt * TL
                x = xin[it % 2]; c = xc[it % 2]; r_ = rg[it % 2]; i_ = ig[it % 2]
                a = aa[it % 2]; u = uu[it % 2]; h = hh[it % 2]; g = gg[it % 2]; o = ob[it % 2]
                hprev = hh[(it - 1) % 2]
                fw.dma("sp", x[:], self.lrux[cb * 128:(cb + 1) * 128, 1 + t0:4 + t0 + TL], tw=x)
                fw.dma("sp", g[:], self.lrug[cb * 128:(cb + 1) * 128, t0:t0 + TL], tw=g)
                w = lambda k: parp[:, P_LCW + cb * 4 + k:P_LCW + cb * 4 + k + 1]
                self.V(lambda e: e.tensor_scalar(c[:], x[:, 0:TL], w(0), parp[:, P_LCB + cb:P_LCB + cb + 1], ALU.mult, ALU.add), [x, parp], [c])
                for k in range(1, 4):
                    self.V(lambda e: e.scalar_tensor_tensor(c[:], x[:, k:k + TL], w(k), c[:], ALU.mult, ALU.add), [x, parp, c], [c])
                for j in range(TL // 512):
                    js = slice(j * 512, (j + 1) * 512)
                    p1 = self.nps()
                    self.PE(lambda e: e.matmul(p1[:, :], gw[:, cb, :], c[:, js], start=True, stop=True), [gw, c], [p1])
                    self.A(lambda e: e.activation(r_[:, js], p1[:, :], AF.Sigmoid, bias=parp[:, P_LGB + cb:P_LGB + cb + 1], scale=1.0), [p1, parp], [r_])
                    p2 = self.nps()
                    self.PE(lambda e: e.matmul(p2[:, :], gw[:, 6 + cb, :], c[:, js], start=True, stop=True), [gw, c], [p2])
                    self.A(lambda e: e.activation(i_[:, js], p2[:, :], AF.Sigmoid, bias=parp[:, P_LGB + 6 + cb:P_LGB + 7 + cb], scale=1.0), [p2, parp], [i_])
                self.A(lambda e: e.activation(a[:], r_[:], AF.Exp, scale=self.lruc[:, cb:cb + 1]), [r_, self.lruc], [a])
                self.A(lambda e: e.activation(u[:], r_[:], AF.Exp, scale=self.lruc[:, 6 + cb:7 + cb]), [r_, self.lruc], [u])
                self.A(lambda e: e.activation(u[:], u[:], AF.Sqrt, bias=1.0, scale=-1.0), [u], [u])
                self.G(lambda e: e.tensor_tensor(i_[:], i_[:], c[:], ALU.mult), [i_, c], [i_])
                self.V(lambda e: e.tensor_tensor(u[:], u[:], i_[:], ALU.mult), [u, i_], [u])
                if tt == 0:
                    self.V(lambda e: e.tensor_tensor_scan(h[:], a[:], u[:], 0.0, ALU.mult, ALU.add), [a, u], [h])
                else:
                    self.V(lambda e: e.tensor_tensor_scan(h[:], a[:], u[:], hprev[:, TL - 1:TL], ALU.mult, ALU.add), [a, u, hprev], [h])
                self.G(lambda e: e.tensor_tensor(o[:], h[:], g[:], ALU.mult), [h, g], [o])
                fw.dma("pool", self.brT[1, cb * 128:(cb + 1) * 128, t0:t0 + TL], o[:], tr=o)
                it += 1
        fw.end()

    def ssd(self, l):
        fw, S = self.fw, self.S
        rowp, parp = self.rowp, self.parp
        fw.begin()
        self.rot = [3, 4, 5]
        xin = fw.ring("sx", 2, [128, 515], F32)
        cv = fw.ring("scv", 2, [128, 512], F32)
        xbs = [fw.sb(f"xbs{i}", [128, 512], F32) for i in range(6)]
        bcf = [fw.sb(f"bcf{i}", [128, 512], BF16) for i in range(4)]
        bf32 = [fw.sb(f"bf32{i}", [128, 512], F32) for i in range(2)]
        H = fw.sb("H", [128, 768], F32)
        Hb = fw.sb("Hb", [128, 768], BF16)
        self.G(lambda e: e.memset(H[:], 0.0), [], [H])
        self.G(lambda e: e.memset(Hb[:], 0.0), [], [Hb])
        xtok = fw.ring("xtok", 2, [128, 768], F32)
        btok = fw.ring("btok", 2, [128, 256], BF16)
        dtt = fw.ring("dtt", 2, [128, 96], F32)
        zt = fw.ring("zt", 2, [128, 768], F32)
        xdt = fw.ring("xdt", 2, [128, 768], BF16)
        xw = fw.ring("xw", 2, [128, 768], BF16)
        LeA = fw.sb("LeA", [128, 12, 128], F32)
        decA = fw.sb("decA", [128, 12, 128], F32)
        MtA = fw.sb("MtA", [128, 12, 128], BF16)
        cbm = fw.ring("cbm", 2, [128, 2, 128], F32)
        y = fw.ring("y", 2, [128, 768], F32)
        t1 = fw.ring("t1", 2, [128, 768], F32)
        nrm = fw.ring("nrm", 2, [128, 8], F32)
        otok = fw.ring("sotok", 8, [128, 768], BF16)
        stg = fw.ring("sstg", 2, [128, 6, 512], BF16)
        it = 0
        for sc in range(S // 512):
            t0 = sc * 512
            for cb in range(10):
                x = xin[cb % 2]; c = cv[cb % 2]
                fw.dma("sp", x[:], self.xbc[cb * 128:(cb + 1) * 128, 1 + t0:4 + t0 + 512], tw=x)
                w = lambda k: parp[:, P_SCW + cb * 4 + k:P_SCW + cb * 4 + k + 1]
                self.V(lambda e: e.tensor_scalar(c[:], x[:, 0:512], w(0), parp[:, P_SCB + cb:P_SCB + cb + 1], ALU.mult, ALU.add), [x, parp], [c])
                for k in range(1, 4):
                    self.V(lambda e: e.scalar_tensor_tensor(c[:], x[:, k:k + 512], w(k), c[:], ALU.mult, ALU.add), [x, parp, c], [c])
                if cb < 6:
                    self.A(lambda e: e.activation(xbs[cb][:], c[:], AF.Silu), [c], [xbs[cb]])
                elif cb < 8:
                    self.A(lambda e: e.activation(bf32[cb - 6][:], c[:], AF.Silu), [c], [bf32[cb - 6]])
                    self.G(lambda e: e.tensor_copy(bcf[cb - 6][:], bf32[cb - 6][:]), [bf32[cb - 6]], [bcf[cb - 6]])
                else:
                    self.A(lambda e: e.activation(bcf[cb - 6][:], c[:], AF.Silu), [c], [bcf[cb - 6]])
            for ci in range(4):
                cs = slice(ci * 128, (ci + 1) * 128)
                tok0 = t0 + ci * 128
                xt = xtok[it % 2]; bt = btok[it % 2]; d = dtt[it % 2]; z = zt[it % 2]
                fw.dma("sp", d[:, 0:12], self.dt[tok0:tok0 + 128, :], tw=d)
                fw.dma("sp", z[:], self.z[tok0:tok0 + 128, :], tw=z)
                for half in range(2):
                    p = self.nps()
                    for j in range(3):
                        self.PE(lambda e: e.transpose(p[:, j * 128:(j + 1) * 128], xbs[half * 3 + j][:, cs], self.idf[:]), [xbs[half * 3 + j], self.idf], [p])
                    self.A(lambda e: e.copy(xt[:, half * 384:(half + 1) * 384], p[:, 0:384]), [p], [xt])
                p = self.nps()
                for g in range(2):
                    self.PE(lambda e: e.transpose(p[:, g * 128:(g + 1) * 128], bf32[g][:, cs], self.idf[:]), [bf32[g], self.idf], [p])
                self.V(lambda e: e.tensor_copy(bt[:], p[:, 0:256]), [p], [bt])
                dtv, dtA, acs, dte, ea, cd = d[:, 0:12], d[:, 12:24], d[:, 24:36], d[:, 36:48], d[:, 48:60], d[:, 60:72]
                self.V(lambda e: e.tensor_tensor(dtA, dtv, self.aneg[:], ALU.mult), [d, self.aneg], [d])
                pc = self.nps()
                self.PE(lambda e: e.matmul(pc[:, 0:12], self.tri[:], dtA, start=True, stop=True), [self.tri, d], [pc])
                self.PE(lambda e: e.matmul(pc[:, 16:28], self.ones[:], dtA, start=True, stop=True), [self.ones, d], [pc])
                self.V(lambda e: e.tensor_copy(acs, pc[:, 0:12]), [pc], [d])
                self.A(lambda e: e.activation(ea, pc[:, 0:12], AF.Exp), [pc], [d])
                self.A(lambda e: e.activation(cd, pc[:, 16:28], AF.Exp), [pc], [d])
                self.V(lambda e: e.tensor_tensor(dte, pc[:, 16:28], acs, ALU.subtract), [pc, d], [d])
                self.A(lambda e: e.activation(dte, dte, AF.Exp), [d], [d])
                xd = xdt[it % 2]; xw_ = xw[it % 2]

                def v12(t):
                    return t[:, 0:768].rearrange("p (h d) -> p h d", h=12)

                def b12(a):
                    return a.unsqueeze(2).to_broadcast([128, 12, 64])
                tt1 = t1[it % 2]
                self.V(lambda e: e.tensor_tensor(v12(tt1), v12(xt), b12(dtv), ALU.mult), [xt, d], [tt1])
                self.G(lambda e: e.tensor_copy(xd[:], tt1[:]), [tt1], [xd])
                self.V(lambda e: e.tensor_tensor(v12(xw_), v12(tt1), b12(dte), ALU.mult), [tt1, d], [xw_])
                cm = cbm[it % 2]
                for g in range(2):
                    pcb = self.nps()
                    self.PE(lambda e: e.matmul(pcb[:, 0:128], bcf[g][:, cs], bcf[2 + g][:, cs], start=True, stop=True), [bcf[g], bcf[2 + g]], [pcb])
                    self.V(lambda e: e.tensor_tensor(cm[:, g, :], pcb[:, 0:128], self.tri[:], ALU.mult), [pcb, self.tri], [cm])
                yps = [self.ps[6], self.ps[7]]
                SG = [self.ps[0], self.ps[1], self.ps[2]]
                self.G(lambda e: e.tensor_tensor(LeA[:, :, :], self.mgt[:, :].unsqueeze(1).to_broadcast([128, 12, 128]),
                                                 dtA.unsqueeze(2).to_broadcast([128, 12, 128]), ALU.mult), [self.mgt, d], [LeA])
                for e_ in range(12):
                    sg = SG[e_ // 4]
                    so = (e_ % 4) * 128
                    self.PE(lambda e: e.matmul(sg[:, so:so + 128], LeA[:, e_, :], self.tri[:], start=True, stop=True), [LeA, self.tri], [sg])
                for j in range(3):
                    self.A(lambda e: e.activation(decA[:, 4 * j:4 * j + 4, :].rearrange("p a b -> p (a b)"), SG[j][:, :], AF.Exp), [SG[j]], [decA])
                for g in range(2):
                    self.V(lambda e: e.tensor_tensor(MtA[:, 6 * g:6 * g + 6, :], decA[:, 6 * g:6 * g + 6, :],
                                                     cm[:, g, :].unsqueeze(1).to_broadcast([128, 6, 128]), ALU.mult), [decA, cm], [MtA])
                for e_ in range(12):
                    g = e_ // 6
                    yp = yps[g]
                    eo = (e_ % 6) * 64
                    self.PE(lambda e: e.matmul(yp[:, eo:eo + 64], MtA[:, e_, :], xd[:, e_ * 64:(e_ + 1) * 64], start=True, stop=True), [MtA, xd], [yp])
                yy = y[it % 2]
                for g in range(2):
                    po = self.nps()
                    self.PE(lambda e: e.matmul(po[:, 0:384], bcf[2 + g][:, cs], Hb[:, g * 384:(g + 1) * 384], start=True, stop=True), [bcf[2 + g], Hb], [po])
                    gs = slice(g * 384, (g + 1) * 384)
                    self.V(lambda e: e.tensor_tensor(tt1[:, gs].rearrange("p (h d) -> p h d", h=6), po[:, 0:384].rearrange("p (h d) -> p h d", h=6),
                                                     ea[:, g * 6:(g + 1) * 6].unsqueeze(2).to_broadcast([128, 6, 64]), ALU.mult), [po, d], [tt1])
                    self.V(lambda e: e.tensor_tensor(yy[:, gs], tt1[:, gs], yps[g][:, 0:384], ALU.add), [tt1, yps[g]], [yy])
                for g in range(2):
                    pst = self.nps()
                    gs = slice(g * 384, (g + 1) * 384)
                    self.PE(lambda e: e.matmul(pst[:, 0:384], bt[:, g * 128:(g + 1) * 128], xw_[:, gs], start=True, stop=True), [bt, xw_], [pst])
                    self.V(lambda e: e.tensor_tensor(H[:, gs].rearrange("p (h d) -> p h d", h=6), H[:, gs].rearrange("p (h d) -> p h d", h=6),
                                                     cd[:, g * 6:(g + 1) * 6].unsqueeze(2).to_broadcast([128, 6, 64]), ALU.mult), [H, d], [H])
                    self.V(lambda e: e.tensor_tensor(H[:, gs], H[:, gs], pst[:, 0:384], ALU.add), [H, pst], [H])
                self.G(lambda e: e.tensor_copy(Hb[:], H[:]), [H], [Hb])
                self.G(lambda e: e.tensor_tensor(v12(tt1), v12(xt), b12(rowp[:, R_DSK:R_DSK + 12]), ALU.mult), [xt, rowp], [tt1])
                self.V(lambda e: e.tensor_tensor(yy[:], yy[:], tt1[:], ALU.add), [yy, tt1], [yy])
                self.A(lambda e: e.activation(z[:], z[:], AF.Silu), [z], [z])
                self.V(lambda e: e.tensor_tensor(yy[:], yy[:], z[:], ALU.mult), [yy, z], [yy])
                n_ = nrm[it % 2]
                for g in range(2):
                    gs = slice(g * 384, (g + 1) * 384)
                    self.A(lambda e: e.activation(tt1[:, gs], yy[:, gs], AF.Square, accum_out=n_[:, g:g + 1]), [yy], [tt1, n_])
                self.A(lambda e: e.activation(n_[:, 2:4], n_[:, 0:2], AF.Sqrt, bias=1e-5, scale=1.0 / 384), [n_], [n_])
                self.V(lambda e: e.reciprocal(n_[:, 2:4], n_[:, 2:4]), [n_], [n_])
                ot = otok[it % 8]
                self.V(lambda e: e.tensor_tensor(yy[:, 0:768].rearrange("p (g d) -> p g d", g=2), yy[:, 0:768].rearrange("p (g d) -> p g d", g=2),
                                                 n_[:, 2:4].unsqueeze(2).to_broadcast([128, 2, 384]), ALU.mult), [yy, n_], [yy])
                self.G(lambda e: e.tensor_tensor(ot[:], yy[:], rowp[:, R_SNORM:R_SNORM + 768], ALU.mult), [yy, rowp], [ot])
                if ci == 3:
                    self.store_fm([otok[(it - 3 + j) % 8] for j in range(4)], 3, t0, stg[sc % 2])
                it += 1
        fw.end()
        self.rot = list(range(8))

    def merge_out(self, l):
        fw, S = self.fw, self.S
        fw.begin()
        TT = 512
        brt = [fw.ring(f"brt{b}", 2, [128, 6, TT], BF16) for b in range(4)]
        wb = fw.ring("wbr", 3, [128, 6, 128], BF16)
        gt = fw.ring("gt", 3, [128, TT], BF16)
        acc = fw.ring("acc", 2, [128, TT], F32)
        tmp = fw.ring("mtmp", 2, [128, TT], F32)
        mT = fw.ring("mT", 2, [128, 16, TT], BF16)
        wo = fw.ring("wo", 2, [128, 16, 512], BF16)
        xr = fw.ring("xr", 2, [128, D], F32)
        wi = 0
        for tt in range(S // TT):
            t0 = tt * TT
            m_ = mT[tt % 2]
            for b in range(4):
                t = brt[b][tt % 2]
                fw.dma("sp", t[:], self.brT[b, :, t0:t0 + TT].rearrange("(kc p) t -> p kc t", p=128), tw=t)
            for cc in range(16):
                a_ = acc[cc % 2]
                for b in range(4):
                    w = wb[wi % 3]; g = gt[wi % 3]
                    wi += 1
                    fw.dma("sp", w[:], self.wb_br[l][b, :, cc * 128:(cc + 1) * 128].rearrange("(kc p) c -> p kc c", p=128), tw=w)
                    fw.dma("sp", g[:], self.gate[b][cc * 128:(cc + 1) * 128, t0:t0 + TT], tw=g)
                    p = self.nps()
                    for kc in range(6):
                        self.PE(lambda e: e.matmul(p[:, :], w[:, kc, :], brt[b][tt % 2][:, kc, :], start=(kc == 0), stop=(kc == 5)),
                                [w, brt[b][tt % 2]], [p], track=(kc == 5))
                    if b == 0:
                        self.V(lambda e: e.tensor_tensor(a_[:], p[:, :], g[:], ALU.mult), [p, g], [a_])
                    else:
                        t_ = tmp[wi % 2]
                        self.V(lambda e: e.tensor_tensor(t_[:], p[:, :], g[:], ALU.mult), [p, g], [t_])
                        if b < 3:
                            self.G(lambda e: e.tensor_tensor(a_[:], a_[:], t_[:], ALU.add), [a_, t_], [a_])
                        else:
                            self.G(lambda e: e.tensor_tensor(m_[:, cc, :], a_[:], t_[:], ALU.add), [a_, t_], [m_])
            xts = {}
            for nq in range(4):
                w = wo[nq % 2]
                fw.dma("sp", w[:], self.wb_out[l][:, nq * 512:(nq + 1) * 512].rearrange("(kc p) c -> p kc c", p=128), tw=w)
                for s in range(4):
                    p = self.nps()
                    for kc in range(16):
                        self.PE(lambda e: e.matmul(p[:, :], m_[:, kc, s * 128:(s + 1) * 128], w[:, kc, :], start=(kc == 0), stop=(kc == 15)),
                                [m_, w], [p], track=(kc == 15))
                    xt = xr[(nq * 4 + s) % 2]
                    r0 = t0 + s * 128
                    fw.dma("sp", xt[:, 0:512], self.xres[r0:r0 + 128, nq * 512:(nq + 1) * 512], tw=xt)
                    self.V(lambda e: e.tensor_tensor(xt[:, 0:512], xt[:, 0:512], p[:, :], ALU.add), [xt, p], [xt])
                    fw.dma("pool", self.xres[r0:r0 + 128, nq * 512:(nq + 1) * 512], xt[:, 0:512], tr=xt)
        fw.end()

    def ffn(self, l):
        fw, S = self.fw, self.S
        fw.begin()
        TT = 512
        self.rot = list(range(4))
        hTs = fw.ring("fhT", 2, [128, 16, TT], BF16)
        xr = fw.ring("fxr", 2, [128, D], F32)
        hb = fw.ring("fhb", 4, [128, D], BF16)
        sm = fw.ring("fsm", 2, [128, 2], F32)
        w1 = fw.ring("w1", 3, [128, 16, 256], BF16)
        w2 = fw.ring("w2", 2, [128, 11, 512], BF16)
        aT = fw.ring("aT", 1, [128, 44, TT], BF16)
        sg = fw.ring("sg", 2, [128, TT], F32)
        xo = fw.ring("xo", 2, [128, 512], F32)
        wi = 0
        for tt in range(S // TT):
            t0 = tt * TT
            hT = hTs[tt % 2]
            a_ = aT[0]
            self.build_hT(hT, t0, 4, R_FFN, xr, hb, sm)
            for fc2 in range(22):
                wg = w1[wi % 3]; wi += 1
                wu = w1[wi % 3]; wi += 1
                fw.dma("sp", wg[:], self.wb_f1[l][:, fc2 * 256:(fc2 + 1) * 256].rearrange("(kc p) c -> p kc c", p=128), tw=wg)
                fw.dma("sp", wu[:], self.wb_f1[l][:, FF + fc2 * 256:FF + (fc2 + 1) * 256].rearrange("(kc p) c -> p kc c", p=128), tw=wu)
                for j in range(2):
                    fc = fc2 * 2 + j
                    pg = self.nps()
                    for kc in range(16):
                        self.PE(lambda e: e.matmul(pg[:, :], wg[:, kc, j * 128:(j + 1) * 128], hT[:, kc, :], start=(kc == 0), stop=(kc == 15)),
                                [wg, hT], [pg], track=(kc == 15))
                    pu = self.nps()
                    for kc in range(16):
                        self.PE(lambda e: e.matmul(pu[:, :], wu[:, kc, j * 128:(j + 1) * 128], hT[:, kc, :], start=(kc == 0), stop=(kc == 15)),
                                [wu, hT], [pu], track=(kc == 15))
                    s_ = sg[fc % 2]
                    self.A(lambda e: e.activation(s_[:], pg[:, :], AF.Silu), [pg], [s_])
                    self.V(lambda e: e.tensor_tensor(a_[:, fc, :], s_[:], pu[:, :], ALU.mult), [s_, pu], [a_])
            for nq in range(4):
                ps4 = [self.ps[4 + j] for j in range(4)]
                for kq in range(4):
                    w = w2[(nq * 4 + kq) % 2]
                    fw.dma("sp", w[:], self.wb_f2[l][kq * 1408:(kq + 1) * 1408, nq * 512:(nq + 1) * 512].rearrange("(kc p) c -> p kc c", p=128), tw=w)
                    for s in range(4):
                        for kc in range(11):
                            last = (kq == 3 and kc == 10)
                            self.PE(lambda e: e.matmul(ps4[s][:, :], a_[:, kq * 11 + kc, s * 128:(s + 1) * 128], w[:, kc, :],
                                                       start=(kq == 0 and kc == 0), stop=last),
                                    [a_, w], [ps4[s]], track=(last or kc == 10))
                for s in range(4):
                    xt = xo[s % 2]
                    r0 = t0 + s * 128
                    fw.dma("sp", xt[:], self.xres[r0:r0 + 128, nq * 512:(nq + 1) * 512], tw=xt)
                    self.V(lambda e: e.tensor_tensor(xt[:], xt[:], ps4[s][:, :], ALU.add), [xt, ps4[s]], [xt])
                    fw.dma("pool", self.xres[r0:r0 + 128, nq * 512:(nq + 1) * 512], xt[:], tr=xt)
        fw.end()
        self.rot = list(range(8))

    def build(self, stop_after=None):
        self.consts()
        self.prepass()
        for l in range(self.L):
            self.load_params(l)
            self.phase_proj(l)
            if stop_after == "proj":
                break
            for gi, (win, dil) in enumerate(DILS):
                self.attention(self.qa[:, gi * 768:(gi + 1) * 768], self.ka[:, gi * 768:(gi + 1) * 768],
                               self.va[:, gi * 768:(gi + 1) * 768], 6, dil, gi * 6, "A", gi)
            self.merge_A()
            self.attention(self.qc, self.kc, self.vc, 2, 1, 18, "C", 0)
            self.lru(l)
            self.ssd(l)
            if stop_after == "mix":
                break
            self.merge_out(l)
            self.ffn(l)
        self.fw.barrier()
        return self.nc


def host_inputs(inp, L):
    f = lambda a: np.ascontiguousarray(np.asarray(a, dtype=np.float32))
    prow = np.zeros((L, NR), np.float32)
    prow[:, R_MIX:R_MIX + D] = inp["mix_norm"]
    prow[:, R_FFN:R_FFN + D] = inp["ffn_norm"]
    prow[:, R_QA:R_QA + 128] = inp["qk_norm_a"][:, 0]
    prow[:, R_KA:R_KA + 128] = inp["qk_norm_a"][:, 1]
    prow[:, R_QC:R_QC + 128] = inp["qk_norm_c"][:, 0]
    prow[:, R_KC:R_KC + 128] = inp["qk_norm_c"][:, 1]
    prow[:, R_SINK:R_SINK + 6] = inp["sinks_c"]
    prow[:, R_DTB:R_DTB + 12] = inp["ssm_dt_bias"]
    prow[:, R_ALOG:R_ALOG + 12] = inp["ssm_a_log"]
    prow[:, R_DSK:R_DSK + 12] = inp["ssm_d"]
    prow[:, R_SNORM:R_SNORM + 768] = inp["ssm_norm"]
    ppar = np.zeros((L, 128, NP_), np.float32)
    ppar[:, :, P_LCW:P_LCW + 24] = np.asarray(inp["conv_lru_w"]).reshape(L, 4, 6, 128).transpose(0, 3, 2, 1).reshape(L, 128, 24)
    ppar[:, :, P_LCB:P_LCB + 6] = np.asarray(inp["conv_lru_b"]).reshape(L, 6, 128).transpose(0, 2, 1)
    ppar[:, :, P_LGB:P_LGB + 12] = np.asarray(inp["lru_gate_b"]).reshape(L, 2, 6, 128).transpose(0, 3, 1, 2).reshape(L, 128, 12)
    ppar[:, :, P_LAM:P_LAM + 6] = np.asarray(inp["lru_lambda"]).reshape(L, 6, 128).transpose(0, 2, 1)
    ppar[:, :, P_SCW:P_SCW + 40] = np.asarray(inp["ssm_conv_w"]).reshape(L, 4, 10, 128).transpose(0, 3, 2, 1).reshape(L, 128, 40)
    ppar[:, :, P_SCB:P_SCB + 10] = np.asarray(inp["ssm_conv_b"]).reshape(L, 10, 128).transpose(0, 2, 1)
    common = {
        "w_in": f(inp["w_in"]), "w_branch": f(inp["w_branch"]), "w_out": f(inp["w_out"]),
        "w_ffn_in": f(inp["w_ffn_in"]), "w_ffn_out": f(inp["w_ffn_out"]),
        "lru_gate_w": f(inp["lru_gate_w"]), "prow": prow, "ppar": ppar,
        "btab": _bias_tables(np.asarray(inp["rel_bias_table"], dtype=np.float32)),
    }
    return common


LAYERS_PER_LAUNCH = 1


def kernel(**inputs):
    x = np.asarray(inputs["x"], dtype=np.float32)
    B, S, _ = x.shape
    L = inputs["w_in"].shape[0]
    common = host_inputs(inputs, L)
    LP = LAYERS_PER_LAUNCH
    nc = K(S, LP).build()
    per_layer = ("w_in", "w_branch", "w_out", "w_ffn_in", "w_ffn_out", "lru_gate_w", "prow", "ppar")
    cur = [np.ascontiguousarray(x[b]) for b in range(B)]
    for l0 in range(0, L, LP):
        cm = {k: (np.ascontiguousarray(v[l0:l0 + LP]) if k in per_layer else v) for k, v in common.items()}
        in_maps = [dict(cm, x=cur[b]) for b in range(B)]
        res = run_bass_kernel_spmd(nc, in_maps, core_ids=list(range(B)))
        cur = [np.ascontiguousarray(np.asarray(res.results[b]["out"], dtype=np.float32)) for b in range(B)]
    return np.stack(cur, axis=0)
```
